# Optimizing a Trainium2 kernel written in Bass

```python
import numpy as np
import jax, jax.numpy as jnp
from jax import lax

D_MODEL = 1024
BATCH = 8
SEQ = 4096
DEPTH = 2

MEM_LEN = 256
NSA_HEADS = 8
NSA_KV_HEADS = 2
NSA_GROUP = NSA_HEADS // NSA_KV_HEADS
NSA_HEAD_DIM = D_MODEL // 16
NSA_WIDTH = NSA_HEADS * NSA_HEAD_DIM
N_BRANCH = 3
CMP_LEN = 32
CMP_STRIDE = 16
CMP_HIDDEN = 2 * NSA_HEAD_DIM
SEL_BLOCK = 64
SEL_TOP = 16
WINDOW = 512
NSA_Q_BLOCK = 64
FORCE_BONUS = 100.0
KV_WIDTH = N_BRANCH * 2 * NSA_KV_HEADS * NSA_HEAD_DIM
GATE_WIDTH = NSA_HEADS * N_BRANCH
POOL_WINDOWS = (2, 4, 8, 16)
POOL_GROUPS = len(POOL_WINDOWS)
POOL_WIDTH = D_MODEL // 2
POOL_GROUP_DIM = POOL_WIDTH // POOL_GROUPS
AB_IN = NSA_WIDTH + KV_WIDTH + GATE_WIDTH + POOL_WIDTH
AB_MIX = NSA_WIDTH + POOL_WIDTH
SG_CHUNK = 128
SG_GROUPS = 8
SG_WIDTH = D_MODEL
SG_GROUP_DIM = SG_WIDTH // SG_GROUPS
MEM_HEADS = 4
MEM_HEAD_DIM = D_MODEL // MEM_HEADS
D_FF = 256 * ((8 * D_MODEL // 3 + 255) // 256)
CONV_WIDTH = 3
ALPHA = (2 * DEPTH) ** 0.25
BETA = (8 * DEPTH) ** -0.25
LN_EPS = 1e-5
NEG_INF = -1e30
N_EVEN = (DEPTH + 1) // 2
N_ODD = DEPTH // 2

kernel_name = "hybrid_nsa_pool_sgu_deepnorm"


def layer_norm(x, g, b):
    xf = x.astype(jnp.float32)
    mu = jnp.mean(xf, -1, keepdims=True)
    var = jnp.mean(jnp.square(xf - mu), -1, keepdims=True)
    return ((xf - mu) * lax.rsqrt(var + LN_EPS) * g + b).astype(x.dtype)


def _gather_blocks(blocks, idx):
    return jax.vmap(jax.vmap(lambda blk, ix: blk[ix]))(blocks, idx)


def nsa_mixer(q, kv, gates, cmp_pe, cmp_w1, cmp_w2):
    B, S = q.shape[0], q.shape[1]
    G, R, dh = NSA_KV_HEADS, NSA_GROUP, NSA_HEAD_DIM
    dt = q.dtype
    n_cmp = (S - CMP_LEN) // CMP_STRIDE + 1
    n_sel = S // SEL_BLOCK
    n_top = min(SEL_TOP, n_sel)
    n_qb = S // NSA_Q_BLOCK
    qh = (q * dh ** -0.5).reshape(B, S, G, R, dh).transpose(0, 2, 3, 1, 4)
    gate_h = jax.nn.sigmoid(gates).reshape(B, S, G, R, N_BRANCH).transpose(0, 2, 3, 1, 4)

    blk_idx = np.arange(n_cmp)[:, None] * CMP_STRIDE + np.arange(CMP_LEN)[None, :]
    blocks = kv[:, :, 0][:, blk_idx]
    blocks = blocks + cmp_pe.transpose(1, 0, 2)[None, None, :, :, None, :]
    blocks = blocks.transpose(0, 3, 4, 1, 2, 5).reshape(B, 2, G, n_cmp, CMP_LEN * dh)
    hid = jax.nn.gelu(jnp.einsum('bkgnf,kfh->bkgnh', blocks, cmp_w1))
    kv_cmp = jnp.einsum('bkgnh,khd->bkgnd', hid, cmp_w2)
    k_cmp, v_cmp = kv_cmp[:, 0], kv_cmp[:, 1]
    cmp_start = np.arange(n_cmp) * CMP_STRIDE
    cmp_end = cmp_start + CMP_LEN - 1
    sel_start = np.arange(n_sel) * SEL_BLOCK
    overlap = jnp.asarray(((cmp_start[:, None] < sel_start[None, :] + SEL_BLOCK)
                           & (cmp_start[:, None] + CMP_LEN > sel_start[None, :])).astype(np.float32))

    kv_s = kv[:, :, 1].transpose(0, 2, 3, 1, 4).reshape(B, 2, G, n_sel, SEL_BLOCK, dh)
    k_sel, v_sel = kv_s[:, 0], kv_s[:, 1]
    kv_w = jnp.pad(kv[:, :, 2].transpose(0, 2, 3, 1, 4), ((0, 0), (0, 0), (0, 0), (WINDOW, 0), (0, 0)))
    k_win, v_win = kv_w[:, 0], kv_w[:, 1]

    j_idx = jnp.arange(n_sel)

    def query_block(c):
        t0 = c * NSA_Q_BLOCK
        tpos = t0 + jnp.arange(NSA_Q_BLOCK)
        qc = lax.dynamic_slice_in_dim(qh, t0, NSA_Q_BLOCK, axis=3)
        gc = lax.dynamic_slice_in_dim(gate_h, t0, NSA_Q_BLOCK, axis=3)
        s = jnp.einsum('bgrqd,bgnd->bgrqn', qc, k_cmp).astype(jnp.float32)
        valid = cmp_end[None, :] <= tpos[:, None]
        p_cmp = jax.nn.softmax(jnp.where(valid, s, NEG_INF), axis=-1) * (tpos >= CMP_LEN - 1)[:, None]
        o_cmp = jnp.einsum('bgrqn,bgnd->bgrqd', p_cmp.astype(dt), v_cmp)
        imp = jnp.einsum('bgrqn,nj->bgqj', p_cmp, overlap)
        cur = tpos // SEL_BLOCK
        blk_ok = j_idx[None, :] * SEL_BLOCK <= tpos[:, None]
        forced = (j_idx[None, :] == 0) | (j_idx[None, :] == cur[:, None]) | (j_idx[None, :] == cur[:, None] - 1)
        score = jnp.where(blk_ok, imp + FORCE_BONUS * forced, NEG_INF)
        _, sel = lax.top_k(score, n_top)
        k_g = _gather_blocks(k_sel, sel)
        v_g = _gather_blocks(v_sel, sel)
        kpos = sel[..., None] * SEL_BLOCK + jnp.arange(SEL_BLOCK)
        s = jnp.einsum('bgrqd,bgqnld->bgrqnl', qc, k_g).astype(jnp.float32)
        s = jnp.where((kpos <= tpos[:, None, None])[:, :, None], s, NEG_INF)
        p = jax.nn.softmax(s.reshape(s.shape[:4] + (-1,)), axis=-1).reshape(s.shape)
        o_sel = jnp.einsum('bgrqnl,bgqnld->bgrqd', p.astype(dt), v_g)
        kw = lax.dynamic_slice_in_dim(k_win, t0, NSA_Q_BLOCK + WINDOW, axis=2)
        vw = lax.dynamic_slice_in_dim(v_win, t0, NSA_Q_BLOCK + WINDOW, axis=2)
        wpos = t0 - WINDOW + jnp.arange(NSA_Q_BLOCK + WINDOW)
        wmask = (wpos[None, :] <= tpos[:, None]) & (wpos[None, :] > tpos[:, None] - WINDOW) & (wpos[None, :] >= 0)
        s = jnp.einsum('bgrqd,bgkd->bgrqk', qc, kw).astype(jnp.float32)
        p = jax.nn.softmax(jnp.where(wmask, s, NEG_INF), axis=-1)
        o_win = jnp.einsum('bgrqk,bgkd->bgrqd', p.astype(dt), vw)
        return gc[..., 0:1] * o_cmp + gc[..., 1:2] * o_sel + gc[..., 2:3] * o_win

    out = lax.map(query_block, jnp.arange(n_qb))
    return out.transpose(1, 0, 4, 2, 3, 5).reshape(B, S, NSA_HEADS * dh)


def pool_mixer(u, pool_w, pool_scale):
    B, S = u.shape[0], u.shape[1]
    uf = u.astype(jnp.float32).reshape(B, S, POOL_GROUPS, POOL_GROUP_DIM)
    t = jnp.arange(S)
    outs = []
    for gi, w in enumerate(POOL_WINDOWS):
        ug = uf[:, :, gi]
        cs = jnp.cumsum(ug, axis=1)
        lag = jnp.pad(cs, ((0, 0), (w, 0), (0, 0)))[:, :S]
        cnt = jnp.minimum(t + 1, w).astype(jnp.float32)[:, None]
        outs.append((cs - lag) / cnt - ug)
    pooled = jnp.stack(outs, axis=2).astype(u.dtype)
    y = jnp.einsum('bsgc,gcd->bsgd', pooled, pool_w).reshape(B, S, POOL_WIDTH)
    return y * pool_scale


def spatial_gating(x, w_in, norm_g, norm_b, w_s, b_s):
    B, S = x.shape[0], x.shape[1]
    z = jax.nn.gelu(x @ w_in)
    u, v = z[..., :SG_WIDTH], z[..., SG_WIDTH:]
    v = layer_norm(v, norm_g, norm_b).reshape(B, S // SG_CHUNK, SG_CHUNK, SG_GROUPS, SG_GROUP_DIM)
    causal = jnp.tril(jnp.ones((SG_CHUNK, SG_CHUNK), w_s.dtype))
    sv = jnp.einsum('hts,bcshd->bcthd', w_s * causal, v) + b_s.T[None, None, :, :, None]
    return u * sv.reshape(B, S, SG_WIDTH)


def memory_attention(x, mem, wq, wkv, wo):
    B, S = x.shape[0], x.shape[1]
    M = mem.shape[1]
    q = (x @ wq).reshape(B, S, MEM_HEADS, MEM_HEAD_DIM) * MEM_HEAD_DIM ** -0.5
    kv = (mem @ wkv).reshape(B, M, 2, MEM_HEADS, MEM_HEAD_DIM)
    s = jnp.einsum('bshd,bmhd->bhsm', q, kv[:, :, 0]).astype(jnp.float32)
    p = jax.nn.softmax(s, axis=-1).astype(x.dtype)
    o = jnp.einsum('bhsm,bmhd->bshd', p, kv[:, :, 1]).reshape(B, S, MEM_HEADS * MEM_HEAD_DIM)
    return o @ wo


def conv_ffn(x, w_up, conv_w, conv_b, w_down):
    h = x @ w_up
    h = lax.conv_general_dilated(h, conv_w[:, None, :], window_strides=(1,), padding=[(CONV_WIDTH - 1, 0)],
                                 dimension_numbers=('NWC', 'WIO', 'NWC'), feature_group_count=2 * D_FF) + conv_b
    a, g = h[..., :D_FF], h[..., D_FF:]
    return (a * jax.nn.silu(g)) @ w_down


def setup_inputs(seed: int = 0) -> dict:
    key = jax.random.key(seed)
    ks = iter(jax.random.split(key, 32))

    def nrm(shape, scale):
        return jax.random.normal(next(ks), shape, jnp.float32) * scale

    D = D_MODEL
    cmp_flat = CMP_LEN * NSA_HEAD_DIM
    return {
        'x': nrm((BATCH, SEQ, D), 1.0),
        'mem': nrm((BATCH, MEM_LEN, D), 1.0),
        'ab_w_in': nrm((N_EVEN, D, AB_IN), D ** -0.5),
        'nsa_cmp_pe': nrm((N_EVEN, 2, CMP_LEN, NSA_HEAD_DIM), 0.1),
        'nsa_cmp_w1': nrm((N_EVEN, 2, cmp_flat, CMP_HIDDEN), cmp_flat ** -0.5),
        'nsa_cmp_w2': nrm((N_EVEN, 2, CMP_HIDDEN, NSA_HEAD_DIM), CMP_HIDDEN ** -0.5),
        'pool_w': nrm((N_EVEN, POOL_GROUPS, POOL_GROUP_DIM, POOL_GROUP_DIM), POOL_GROUP_DIM ** -0.5),
        'pool_scale': 1.0 + nrm((N_EVEN, POOL_WIDTH), 0.1),
        'ab_w_out': nrm((N_EVEN, AB_MIX, D), BETA * AB_MIX ** -0.5),
        'sg_w_in': nrm((N_ODD, D, 2 * SG_WIDTH), D ** -0.5),
        'sg_norm_g': 1.0 + nrm((N_ODD, SG_WIDTH), 0.1),
        'sg_norm_b': nrm((N_ODD, SG_WIDTH), 0.1),
        'sg_w_s': nrm((N_ODD, SG_GROUPS, SG_CHUNK, SG_CHUNK), SG_CHUNK ** -0.5),
        'sg_b_s': 1.0 + nrm((N_ODD, SG_GROUPS, SG_CHUNK), 0.1),
        'sg_w_out': nrm((N_ODD, SG_WIDTH, D), BETA * SG_WIDTH ** -0.5),
        'ln_g': 1.0 + nrm((DEPTH, 3, D), 0.1),
        'ln_b': nrm((DEPTH, 3, D), 0.1),
        'mem_wq': nrm((DEPTH, D, MEM_HEADS * MEM_HEAD_DIM), D ** -0.5),
        'mem_wkv': nrm((DEPTH, D, 2 * MEM_HEADS * MEM_HEAD_DIM), D ** -0.5),
        'mem_wo': nrm((DEPTH, MEM_HEADS * MEM_HEAD_DIM, D), BETA * D ** -0.5),
        'ffn_w_up': nrm((DEPTH, D, 2 * D_FF), D ** -0.5),
        'ffn_conv_w': nrm((DEPTH, CONV_WIDTH, 2 * D_FF), CONV_WIDTH ** -0.5),
        'ffn_conv_b': nrm((DEPTH, 2 * D_FF), 0.1),
        'ffn_w_down': nrm((DEPTH, D_FF, D), BETA * D_FF ** -0.5),
    }


def reference(x, mem, ab_w_in, nsa_cmp_pe, nsa_cmp_w1, nsa_cmp_w2, pool_w, pool_scale, ab_w_out,
              sg_w_in, sg_norm_g, sg_norm_b, sg_w_s, sg_b_s, sg_w_out, ln_g, ln_b,
              mem_wq, mem_wkv, mem_wo, ffn_w_up, ffn_conv_w, ffn_conv_b, ffn_w_down):
    B, S = x.shape[0], x.shape[1]
    o_q = NSA_WIDTH
    o_kv = o_q + KV_WIDTH
    o_g = o_kv + GATE_WIDTH
    for layer in range(DEPTH):
        i = layer // 2
        if layer % 2 == 0:
            proj = x @ ab_w_in[i]
            q = proj[..., :o_q].reshape(B, S, NSA_HEADS, NSA_HEAD_DIM)
            kv = proj[..., o_q:o_kv].reshape(B, S, N_BRANCH, 2, NSA_KV_HEADS, NSA_HEAD_DIM)
            gates = proj[..., o_kv:o_g].reshape(B, S, NSA_HEADS, N_BRANCH)
            u = proj[..., o_g:]
            mixed = jnp.concatenate([
                nsa_mixer(q, kv, gates, nsa_cmp_pe[i], nsa_cmp_w1[i], nsa_cmp_w2[i]),
                pool_mixer(u, pool_w[i], pool_scale[i]),
            ], axis=-1)
            y = mixed @ ab_w_out[i]
        else:
            y = spatial_gating(x, sg_w_in[i], sg_norm_g[i], sg_norm_b[i], sg_w_s[i], sg_b_s[i]) @ sg_w_out[i]
        x = layer_norm(ALPHA * x + y, ln_g[layer, 0], ln_b[layer, 0])
        x = layer_norm(ALPHA * x + memory_attention(x, mem, mem_wq[layer], mem_wkv[layer], mem_wo[layer]),
                       ln_g[layer, 1], ln_b[layer, 1])
        x = layer_norm(ALPHA * x + conv_ffn(x, ffn_w_up[layer], ffn_conv_w[layer], ffn_conv_b[layer], ffn_w_down[layer]),
                       ln_g[layer, 2], ln_b[layer, 2])
    return x
```

```python
import numpy as np

from concourse.bass_utils import run_bass_kernel_spmd

import contextlib
import concourse.bass as bass
import concourse.mybir as mybir

F32 = mybir.dt.float32
BF16 = mybir.dt.bfloat16
I32 = mybir.dt.int32
AF = mybir.ActivationFunctionType
ALU = mybir.AluOpType
AX = mybir.AxisListType

ENGS = ("tensor", "vector", "scalar", "gpsimd", "sync")
SEM_CHUNK = 12000
N_DMA_SEMS = 12


def _key(ap):
    if isinstance(ap, (str, tuple)):
        return ap
    t = getattr(ap, "tensor", None)
    return t.name if t is not None else ap.name


class Op:
    __slots__ = ("eng", "fn", "deps", "is_dma", "idx", "eidx", "milestone",
                 "sem", "val", "pe_mm", "dma_prev")

    def __init__(self, eng, fn, is_dma, pe_mm):
        self.eng = eng
        self.fn = fn
        self.deps = set()
        self.is_dma = is_dma
        self.milestone = False
        self.sem = None
        self.val = 0
        self.pe_mm = pe_mm
        self.dma_prev = None


class Sched:
    def __init__(self, nc):
        self.nc = nc
        self.ops = []
        self.last_w = {}
        self.readers = {}
        self.stack = contextlib.ExitStack()
        self.dma_rr = {e: 0 for e in ENGS}
        self.dma_last = {}
        self.n_alloc = 0

    def sb(self, name, shape, dtype):
        self.n_alloc += 1
        return self.stack.enter_context(self.nc.sbuf_tensor(f"{name}_u{self.n_alloc}", list(shape), dtype))

    def ps(self, name, shape, dtype=F32):
        return self.stack.enter_context(self.nc.psum_tensor(name, list(shape), dtype))

    @contextlib.contextmanager
    def scope(self):
        outer = self.stack
        inner = contextlib.ExitStack()
        self.stack = inner
        try:
            yield
        finally:
            self.barrier()
            inner.close()
            self.stack = outer

    def add(self, eng, fn, reads=(), writes=(), is_dma=False, pe_mm=False):
        op = Op(eng, fn, is_dma, pe_mm)
        op.idx = len(self.ops)
        rk = [_key(r) for r in reads if r is not None]
        wk = [_key(w) for w in writes if w is not None]
        for k in rk:
            if isinstance(k, str) and k.startswith("bank") and k not in wk:
                wk.append(k)
        for k in rk:
            lw = self.last_w.get(k)
            if lw is not None:
                op.deps.add(lw)
        for k in wk:
            lw = self.last_w.get(k)
            if lw is not None:
                op.deps.add(lw)
            for r in self.readers.get(k, ()):
                op.deps.add(r)
        for k in rk:
            self.readers.setdefault(k, []).append(op.idx)
        for k in wk:
            self.last_w[k] = op.idx
            self.readers[k] = []
        op.deps.discard(op.idx)
        if is_dma:
            slot = self.dma_rr[eng] % N_DMA_SEMS
            self.dma_rr[eng] += 1
            prev = self.dma_last.get((eng, slot))
            op.sem = (eng, slot)
            op.val = (prev.val if prev is not None else 0) + 16
            op.dma_prev = prev
            self.dma_last[(eng, slot)] = op
        self.ops.append(op)
        return op

    def barrier(self):
        last = {}
        for op in self.ops:
            if op.fn is None:
                continue
            if op.is_dma:
                last[("dma",) + op.sem] = op.idx
            else:
                last[op.eng] = op.idx
        deps = set(last.values())
        for e in ENGS:
            op = self.add(e, None)
            op.deps |= deps
        self.last_w = {}
        self.readers = {}

    def emit(self):
        nc = self.nc
        ops = self.ops
        for op in ops:
            nd = set()
            for d in op.deps:
                dop = ops[d]
                assert dop.fn is not None
                if (not dop.is_dma) and dop.eng == op.eng and op.eng == "tensor":
                    continue
                nd.add(d)
            op.deps = nd
        for op in ops:
            for d in op.deps:
                if not ops[d].is_dma:
                    ops[d].milestone = True
        cnt = {e: 0 for e in ENGS}
        for op in ops:
            if op.is_dma or op.fn is None:
                continue
            if op.milestone:
                c = cnt[op.eng]
                op.sem = (op.eng, "c", c // SEM_CHUNK)
                op.val = c % SEM_CHUNK + 1
                cnt[op.eng] = c + 1
        sem_names = set()
        for op in ops:
            if op.sem is not None and (op.is_dma or op.milestone):
                sem_names.add(op.sem)
        sems = {}
        for sn in sorted(sem_names, key=str):
            sems[sn] = self.stack.enter_context(
                nc.semaphore("s_" + "_".join(str(x) for x in sn)))
        self.n_sems = len(sems)
        per_eng = {e: [op for op in ops if op.eng == e] for e in ENGS}
        stats = {"waits": 0, "insts": 0}

        def run(e, eng):
            waited = {}

            def wait(sn, val):
                if waited.get(sn, 0) >= val:
                    return
                eng.wait_ge(sems[sn], val)
                waited[sn] = val
                stats["waits"] += 1

            for op in per_eng[e]:
                for d in sorted(op.deps):
                    dop = ops[d]
                    wait(dop.sem, dop.val)
                if op.is_dma and op.dma_prev is not None:
                    wait(op.sem, op.dma_prev.val)
                if op.fn is None:
                    continue
                inst = op.fn(eng)
                stats["insts"] += 1
                if op.is_dma:
                    inst.then_inc(sems[op.sem], 16)
                elif op.milestone:
                    inst.then_inc(sems[op.sem], 1)
            for (de, slot), lop in self.dma_last.items():
                if de == e:
                    wait(lop.sem, lop.val)

        with nc.Block() as block:
            @block.sync
            def _(eng):
                run("sync", eng)

            @block.scalar
            def _(eng):
                run("scalar", eng)

            @block.gpsimd
            def _(eng):
                run("gpsimd", eng)

            @block.vector
            def _(eng):
                run("vector", eng)

            @block.tensor
            def _(eng):
                run("tensor", eng)
        self.stats = stats
        self.stack.close()

    def dma(self, out, in_, eng="sync", extra_r=(), extra_w=(), **kw):
        return self.add(eng, lambda e: e.dma_start(out=out, in_=in_, **kw),
                        reads=[in_, *extra_r], writes=[out, *extra_w], is_dma=True)

    def mm(self, out, lhsT, rhs, start=True, stop=True, **kw):
        return self.add("tensor",
                        lambda e: e.matmul(out, lhsT, rhs, start=start, stop=stop, **kw),
                        reads=[lhsT, rhs] + ([] if start else [out]), writes=[out], pe_mm=True)

    def tr(self, out, in_, ident):
        return self.add("tensor", lambda e: e.transpose(out, in_, ident),
                        reads=[in_, ident], writes=[out], pe_mm=True)

    def act(self, out, in_, func, bias=None, scale=None, accum_out=None, eng="scalar"):
        kw = {}
        rd = [in_]
        if bias is not None:
            kw["bias"] = bias
            if not isinstance(bias, (int, float)):
                rd.append(bias)
        if scale is not None:
            kw["scale"] = scale
            if not isinstance(scale, (int, float)):
                rd.append(scale)
        wr = [out]
        if accum_out is not None:
            kw["accum_out"] = accum_out
            wr.append(accum_out)
        return self.add("scalar", lambda e: e.activation(out, in_, func, **kw),
                        reads=rd, writes=wr)

    def tt(self, out, in0, in1, op, eng="vector"):
        return self.add(eng, lambda e: e.tensor_tensor(out, in0, in1, op),
                        reads=[in0, in1], writes=[out])

    def ts(self, out, in0, s1, s2=None, op0=ALU.mult, op1=None, eng="vector", accum_out=None):
        rd = [in0]
        if not isinstance(s1, (int, float)):
            rd.append(s1)
        if s2 is not None and not isinstance(s2, (int, float)):
            rd.append(s2)
        kw = {}
        if op1 is not None:
            kw["op1"] = op1
        wr = [out]
        if accum_out is not None:
            kw["accum_out"] = accum_out
            wr.append(accum_out)
        return self.add(eng, lambda e: e.tensor_scalar(out, in0, s1, s2, op0, **kw),
                        reads=rd, writes=wr)

    def stt(self, out, in0, scalar, in1, op0, op1, eng="vector"):
        rd = [in0, in1]
        if not isinstance(scalar, (int, float)):
            rd.append(scalar)
        return self.add(eng, lambda e: e.scalar_tensor_tensor(out, in0, scalar, in1, op0, op1),
                        reads=rd, writes=[out])

    def copy(self, out, in_, eng="vector"):
        if eng == "scalar":
            return self.add(eng, lambda e: e.copy(out, in_), reads=[in_], writes=[out])
        return self.add(eng, lambda e: e.tensor_copy(out, in_), reads=[in_], writes=[out])

    def memset(self, ap, val, eng="vector"):
        return self.add(eng, lambda e: e.memset(ap, val), writes=[ap])

    def generic(self, eng, fn, reads=(), writes=()):
        return self.add(eng, fn, reads=reads, writes=writes)


SEQ = 4096
DM = 1024
NT = SEQ // 128
TB = 512
NTB = SEQ // TB
DFF = 2816
NJ = DFF // 128
ALPHA_C = 4.0 ** 0.25
LN_EPS_C = 1e-5
NEGB = -30000.0
W_IN_COLS = 1816


class Ctx:
    pass


def _is_dram(ap):
    return str(getattr(ap, "space", "")) == "DRAM" or "DRam" in type(getattr(ap, "tensor", ap)).__name__


def dma(S, out, in_, eng="sync", rkey=None, wkey=None, **kw):
    def dk(ap, k):
        if k is not None:
            return k
        n = ap.tensor.name
        return ("W", n) if n.startswith("s_w_") else None
    r = dk(in_, rkey) if _is_dram(in_) else in_
    w = dk(out, wkey) if _is_dram(out) else out
    return S.add(eng, lambda e: e.dma_start(out=out, in_=in_, **kw),
                 reads=[r], writes=[w], is_dma=True)


def build_program(nc, phases, debug_out=None):
    S = Sched(nc)
    C = Ctx()
    C.S = S
    C.nc = nc

    def din(name, shape):
        return nc.dram_tensor(name, list(shape), F32, kind="ExternalInput").ap()

    def dscr(name, shape, dt=BF16):
        return nc.dram_tensor(name, list(shape), dt, kind="Internal").ap()

    I = {}
    I["x"] = din("x", [SEQ, DM])
    I["xT"] = din("xT", [DM, SEQ])
    I["memT"] = din("memT", [DM, 256])
    I["ab_w_in"] = din("ab_w_in", [DM, W_IN_COLS])
    I["cmp_peT"] = din("cmp_peT", [2, 64, 32])
    I["cmp_w1"] = din("cmp_w1", [2, 2048, 128])
    I["cmp_w2"] = din("cmp_w2", [2, 128, 64])
    I["pool_w"] = din("pool_w", [4, 128, 128])
    I["pool_scaleT"] = din("pool_scaleT", [128, 4])
    I["ab_w_out"] = din("ab_w_out", [DM, DM])
    I["sg_w_in"] = din("sg_w_in", [DM, 2 * DM])
    I["sg_norm_g"] = din("sg_norm_g", [1, DM])
    I["sg_norm_b"] = din("sg_norm_b", [1, DM])
    I["sg_w_sT"] = din("sg_w_sT", [8, 128, 128])
    I["sg_b_sT"] = din("sg_b_sT", [128, 8])
    I["sg_w_out"] = din("sg_w_out", [DM, DM])
    I["ln_g"] = din("ln_g", [6, DM])
    I["ln_b"] = din("ln_b", [6, DM])
    I["mem_wq"] = din("mem_wq", [2, DM, DM])
    I["mem_wkv"] = din("mem_wkv", [2, DM, 2 * DM])
    I["mem_wo"] = din("mem_wo", [2, DM, DM])
    I["ffn_w_up"] = din("ffn_w_up", [2, DM, 2 * DFF])
    I["ffn_cwb"] = din("ffn_cwb", [2, 128, 2 * NJ, 4])
    I["ffn_w_down"] = din("ffn_w_down", [2, DFF, DM])
    out_d = nc.dram_tensor("out", [SEQ, DM], F32, kind="ExternalOutput").ap()
    C.I = I

    D = {}
    D["xT0"] = dscr("s_xT0", [DM, SEQ])
    D["xT1"] = dscr("s_xT1", [DM, SEQ])
    D["xa"] = dscr("s_xa", [SEQ, DM], F32)
    D["xb"] = dscr("s_xb", [SEQ, DM], F32)
    D["memT"] = dscr("s_memT", [DM, 256])
    D["ab_w_in"] = dscr("s_w_ab_w_in", [DM, W_IN_COLS])
    D["cmp_peT"] = dscr("s_w_cmp_peT", [2, 64, 32])
    D["cmp_w1"] = dscr("s_w_cmp_w1", [2, 2048, 128])
    D["cmp_w2"] = dscr("s_w_cmp_w2", [2, 128, 64])
    D["pool_w"] = dscr("s_w_pool_w", [4, 128, 128])
    D["ab_w_out"] = dscr("s_w_ab_w_out", [DM, DM])
    D["sg_w_in"] = dscr("s_w_sg_w_in", [DM, 2 * DM])
    D["sg_w_out"] = dscr("s_w_sg_w_out", [DM, DM])
    D["mem_wq"] = [dscr(f"s_w_mem_wq{l}", [DM, DM]) for l in range(2)]
    D["mem_wkv"] = [dscr(f"s_w_mem_wkv{l}", [DM, 2 * DM]) for l in range(2)]
    D["mem_wo"] = [dscr(f"s_w_mem_wo{l}", [DM, DM]) for l in range(2)]
    D["ffn_up"] = [dscr(f"s_w_ffn_up{l}", [NJ, 128, 8, 256]) for l in range(2)]
    D["ffn_down"] = [dscr(f"s_w_ffn_down{l}", [DFF, DM]) for l in range(2)]
    D["fm"] = dscr("s_fm", [16, 64, SEQ])
    D["vt"] = dscr("s_vt", [NT, 128, 4 * 65])
    D["gt"] = dscr("s_gt", [NT, 128, 24], F32)
    D["kcT"] = dscr("s_kcT", [2, 64, 256])
    D["vc"] = dscr("s_vc", [2, 2, 128, 65])
    D["selT"] = dscr("s_selT", [2, 64, SEQ])
    D["mixT"] = dscr("s_mixT", [DM, SEQ])
    C.D = D

    C.banks = [S.ps(f"bank{i}", [128, 512], F32) for i in range(8)]
    C.bank_i = 0

    C.bank_cls = None
    C.bank_ci = {}

    def bank(cls=None):
        if cls is not None and C.bank_cls is not None:
            lst = C.bank_cls[cls]
            i = C.bank_ci.get(cls, 0)
            C.bank_ci[cls] = i + 1
            return C.banks[lst[i % len(lst)]]
        b = C.banks[C.bank_i % 8]
        C.bank_i += 1
        return b
    C.bank = bank

    ident_f = S.sb("ident_f", [128, 128], F32)
    C.ident_f = ident_f
    C.ident = S.sb("ident", [128, 128], BF16)
    S.memset(ident_f[:], 0.0, eng="gpsimd")
    S.generic("gpsimd", lambda e: e.affine_select(out=ident_f[:], in_=ident_f[:], pattern=[[-1, 128]],
                                                  compare_op=ALU.not_equal, fill=1.0, base=0, channel_multiplier=1),
              reads=[ident_f], writes=[ident_f])
    S.copy(C.ident[:], ident_f[:], eng="gpsimd")
    C.eps = S.sb("eps_t", [128, 1], F32)
    S.memset(C.eps[:], LN_EPS_C, eng="gpsimd")
    C.neghalf = S.sb("neghalf_t", [128, 1], F32)
    S.memset(C.neghalf[:], -0.5, eng="gpsimd")

    def castcopy(dst, src):
        dma(S, dst, src, eng="gpsimd", max_dma_last_dim=4096)
    castcopy_now = castcopy

    C.cast_pending = []
    C.drip_n = 1

    C.cast_tag = 0

    def castcopy_lazy(dst, src):
        C.cast_pending.append((dst, src, C.cast_tag))

    def drip(n=None, upto=None):
        for _ in range(C.drip_n if n is None else n):
            if C.cast_pending and (upto is None or C.cast_pending[0][2] <= upto):
                d_, s_, _t = C.cast_pending.pop(0)
                castcopy(d_, s_)
    C.drip = drip

    def drip_setup(steps):
        C.drip_n = max(1, -(-len(C.cast_pending) // max(1, int(steps * 0.6))))
    C.drip_setup = drip_setup

    def cast_for(ph, lazy=True):
        castcopy = castcopy_lazy if lazy else castcopy_now
        if ph == "A":
            for r in range(8):
                castcopy(D["ab_w_in"][r * 128:(r + 1) * 128, :], I["ab_w_in"][r * 128:(r + 1) * 128, :])
            castcopy(D["pool_w"], I["pool_w"])
            castcopy(D["cmp_peT"], I["cmp_peT"])
            for k in range(2):
                for r in range(4):
                    castcopy(D["cmp_w1"][k, r * 512:(r + 1) * 512, :], I["cmp_w1"][k, r * 512:(r + 1) * 512, :])
            castcopy(D["cmp_w2"], I["cmp_w2"])
            for r in range(4):
                castcopy(D["ab_w_out"][r * 256:(r + 1) * 256, :], I["ab_w_out"][r * 256:(r + 1) * 256, :])
        elif ph in ("M0", "M1"):
            l = int(ph[1])
            for r in range(4):
                sl = slice(r * 256, (r + 1) * 256)
                castcopy(D["mem_wq"][l][sl, :], I["mem_wq"][l, sl, :])
            for r in range(8):
                sl = slice(r * 128, (r + 1) * 128)
                castcopy(D["mem_wkv"][l][sl, :], I["mem_wkv"][l, sl, :])
            for r in range(4):
                sl = slice(r * 256, (r + 1) * 256)
                castcopy(D["mem_wo"][l][sl, :], I["mem_wo"][l, sl, :])
        elif ph in ("F0", "F1"):
            l = int(ph[1])
            for j in range(NJ):
                for h in range(2):
                    c0 = h * DFF + j * 128
                    castcopy(D["ffn_up"][l][j, :, :, h * 128:(h + 1) * 128],
                             I["ffn_w_up"][l, :, c0:c0 + 128].rearrange("(kc p) c -> p kc c", p=128))
            for r in range(NJ):
                sl = slice(r * 128, (r + 1) * 128)
                castcopy(D["ffn_down"][l][sl, :], I["ffn_w_down"][l, sl, :])
        elif ph == "G":
            for r in range(8):
                sl = slice(r * 128, (r + 1) * 128)
                castcopy(D["sg_w_in"][sl, :], I["sg_w_in"][sl, :])
            for r in range(4):
                sl = slice(r * 256, (r + 1) * 256)
                castcopy(D["sg_w_out"][sl, :], I["sg_w_out"][sl, :])

    for r in range(8):
        castcopy(D["xT0"][r * 128:(r + 1) * 128, :], I["xT"][r * 128:(r + 1) * 128, :])
    castcopy(D["memT"], I["memT"])
    S.barrier()
    cast_for(phases[0], lazy=False)
    for ph_ in phases[1:]:
        cast_for(ph_, lazy=(phases[0] == "A"))

    x_in, xT_in = I["x"], D["xT0"]
    xbufs = [D["xa"], D["xb"]]
    xTbufs = [D["xT1"], D["xT0"]]
    ln_idx = {"A": 0, "M0": 1, "F0": 2, "G": 3, "M1": 4, "F1": 5}
    for pi, ph in enumerate(phases):
        last = pi == len(phases) - 1
        x_out = out_d if last else xbufs[pi % 2]
        xT_out = None if last else xTbufs[pi % 2]
        li = ln_idx[ph]
        C.drip_setup(64)
        if ph == "A":
            phase_nsa(C, x_in, xT_in, x_out, xT_out, li)
        elif ph in ("M0", "M1"):
            phase_mem(C, int(ph[1]), x_in, xT_in, x_out, xT_out, li)
        elif ph in ("F0", "F1"):
            phase_ffn(C, int(ph[1]), x_in, xT_in, x_out, xT_out, li)
        elif ph == "G":
            phase_sgu(C, x_in, xT_in, x_out, xT_out, li)
        C.drip(10 ** 6)
        x_in, xT_in = x_out, xT_out
    S.emit()
    return S


def tail_alloc(C, li, g_eng="vector", d1=1, d2=1, d1b=1, norm_eng="scalar"):
    S = C.S
    T = Ctx()
    T.g_eng = g_eng
    T.gbc = S.sb("t_gbc", [128, DM], F32)
    T.bbc = S.sb("t_bbc", [128, DM], F32)
    dma(S, T.gbc[:], C.I["ln_g"][li:li + 1, :].partition_broadcast(128))
    dma(S, T.bbc[:], C.I["ln_b"][li:li + 1, :].partition_broadcast(128))
    T.NX = 3
    T.xt = [S.sb(f"t_xt{i}", [128, DM], F32) for i in range(T.NX)]
    T.z = [S.sb(f"t_z{i}", [128, DM], F32) for i in range(T.NX)]
    T.xb = [S.sb(f"t_xb{i}", [128, DM], BF16) for i in range(4)]
    T.st = [S.sb(f"t_st{i}", [128, 12], F32) for i in range(T.NX)]
    T.mv = [S.sb(f"t_mv{i}", [128, 4], F32) for i in range(T.NX)]
    T.xTblk = [S.sb(f"t_xTblk{i}", [128, 8, TB], BF16) for i in range(2)]
    T.pending = []
    T.d1, T.d2, T.d1b = d1, d2, d1b
    T.norm_eng = norm_eng
    return T


def tail_load_x(C, T, x_in, tile_i):
    dma(C.S, T.xt[tile_i % T.NX][:], x_in[tile_i * 128:(tile_i + 1) * 128, :])


def tail_tick(C, T):
    for it in T.pending:
        it[0] -= 1
    due = [it for it in T.pending if it[0] <= 0]
    T.pending = [it for it in T.pending if it[0] > 0]
    for it in due:
        it[1]()


def tail_flush(C, T, keep=0):
    while T.pending:
        T.pending.pop(0)[1]()


def tail_tile(C, T, ya, yb, tile_i, x_out, xT_out):
    S = C.S
    k = tile_i % T.NX
    xt, z, st, mv = T.xt[k], T.z[k], T.st[k], T.mv[k]
    xb = T.xb[tile_i % 4]
    S.stt(z[:, 0:512], xt[:, 0:512], ALPHA_C, ya[:], ALU.mult, ALU.add)
    S.stt(z[:, 512:1024], xt[:, 512:1024], ALPHA_C, yb[:], ALU.mult, ALU.add)
    S.generic("vector", lambda e: e.bn_stats(out=st[:, 0:6], in_=z[:, 0:512]), reads=[z], writes=[st])
    S.generic("vector", lambda e: e.bn_stats(out=st[:, 6:12], in_=z[:, 512:1024]), reads=[z], writes=[st])
    S.generic("vector", lambda e: e.bn_aggr(out=mv[:, 0:2], in_=st[:]), reads=[st], writes=[mv])
    S.ts(mv[:, 2:3], mv[:, 1:2], LN_EPS_C, None, op0=ALU.add)

    def part1():
        S.tt(mv[:, 2:3], mv[:, 2:3], C.neghalf[:], ALU.pow, eng="gpsimd")
        S.ts(mv[:, 3:4], mv[:, 0:1], mv[:, 2:3], -1.0, op0=ALU.mult, op1=ALU.mult)
        if T.d1b > 0:
            T.pending.append([T.d1b, part1b])
        else:
            part1b()

    def part1b():
        if T.norm_eng == "vector":
            S.ts(z[:], z[:], mv[:, 2:3], mv[:, 3:4], op0=ALU.mult, op1=ALU.add)
        else:
            S.act(z[:], z[:], AF.Identity, bias=mv[:, 3:4], scale=mv[:, 2:3])
        S.tt(z[:], z[:], T.gbc[:], ALU.mult, eng=T.g_eng)
        S.tt(z[:], z[:], T.bbc[:], ALU.add, eng="gpsimd")
        dma(S, x_out[tile_i * 128:(tile_i + 1) * 128, :], z[:])
        if xT_out is not None:
            T.pending.append([T.d2, part2])

    def part2():
        tb, s = tile_i // 4, tile_i % 4
        blk = T.xTblk[tb % 2]
        S.copy(xb[:], z[:], eng="scalar")
        pt = C.bank()
        ptb = pt[:].bitcast(BF16)
        for kc in range(8):
            S.tr(ptb[:, kc * 128:(kc + 1) * 128], xb[:, kc * 128:(kc + 1) * 128], C.ident[:])
        S.copy(blk[:, :, s * 128:(s + 1) * 128], ptb.rearrange("p (k t) -> p k t", k=8), eng="vector")
        if s == 3:
            dma(S, xT_out[:, tb * TB:(tb + 1) * TB].rearrange("(kc p) t -> p kc t", p=128), blk[:])
    T.pending.append([T.d1, part1])


def load_w(C, name, src, kc, n, eng="sync"):
    S = C.S
    t = S.sb(name, [128, kc, n], BF16)
    half = kc // 2 if kc >= 2 else kc
    dma(S, t[:, 0:half, :], src[0:half * 128, :].rearrange("(kc p) n -> p kc n", p=128), eng=eng)
    if half < kc:
        dma(S, t[:, half:kc, :], src[half * 128:kc * 128, :].rearrange("(kc p) n -> p kc n", p=128), eng=eng)
    return t


def load_xT_blk(C, dst, xT_in, tb):
    dma(C.S, dst[:], xT_in[:, tb * TB:(tb + 1) * TB].rearrange("(kc p) t -> p kc t", p=128))


def phase_mem(C, layer, x_in, xT_in, x_out, xT_out, li):
    S, D = C.S, C.D
    with S.scope():
        T = tail_alloc(C, li, g_eng="gpsimd")
        memT = load_w(C, "m_memT", D["memT"], 8, 256)
        wkv = load_w(C, "m_wkv", D["mem_wkv"][layer], 8, 2 * DM)
        wq = load_w(C, "m_wq", D["mem_wq"][layer], 8, DM, eng="scalar")
        wo = load_w(C, "m_wo", D["mem_wo"][layer], 8, DM, eng="scalar")
        KT = S.sb("m_KT", [128, 8, 256], BF16)
        V = S.sb("m_V", [128, 2, 4, 257], BF16)
        for mc in range(2):
            for h in range(4):
                S.memset(V[:, mc, h, 256:257], 1.0, eng="gpsimd")
        for hd in range(8):
            ps = C.bank()
            for kc in range(8):
                S.mm(ps[:, 0:256], wkv[:, kc, hd * 128:(hd + 1) * 128], memT[:, kc, :], start=kc == 0, stop=kc == 7)
            S.copy(KT[:, hd, :], ps[:, 0:256], eng="vector")
        for mc in range(2):
            for half in range(2):
                ps = C.bank()
                for kc in range(8):
                    S.mm(ps[:], memT[:, kc, mc * 128:(mc + 1) * 128],
                         wkv[:, kc, DM + half * 512:DM + (half + 1) * 512], start=kc == 0, stop=kc == 7)
                for hh in range(2):
                    S.copy(V[:, mc, 2 * half + hh, 0:256], ps[:, hh * 256:(hh + 1) * 256], eng="vector")
        xTb = [S.sb(f"m_xTb{i}", [128, 8, TB], BF16) for i in range(2)]
        qT = S.sb("m_qT", [128, 8, TB], BF16)
        eT = [S.sb(f"m_eT{i}", [128, TB], BF16) for i in range(4)]
        O = [S.sb(f"m_O{i}", [128, DM], BF16) for i in range(4)]
        OT = [S.sb(f"m_OT{i}", [128, 8, 128], BF16) for i in range(2)]
        rl = [S.sb(f"m_rl{i}", [128, 1], F32) for i in range(4)]
        qTs = [qT, S.sb("m_qT2", [128, 8, TB], BF16)]
        load_xT_blk(C, xTb[0], xT_in, 0)
        rli = [0]

        def stA(tb):
            if tb + 1 < NTB:
                load_xT_blk(C, xTb[(tb + 1) % 2], xT_in, tb + 1)
            xt = xTb[tb % 2]
            qq = qTs[tb % 2]
            for hd in range(8):
                ps = C.bank()
                for kc in range(8):
                    S.mm(ps[:], wq[:, kc, hd * 128:(hd + 1) * 128], xt[:, kc, :], start=kc == 0, stop=kc == 7)
                if hd % 2 == 0:
                    S.act(qq[:, hd, :], ps[:], AF.Copy, scale=1.0 / 16.0)
                else:
                    S.ts(qq[:, hd, :], ps[:], 1.0 / 16.0, None, op0=ALU.mult)

        def stB(tb):
            qq = qTs[tb % 2]

            def scores(h):
                es = []
                for mc in range(2):
                    ps = C.bank()
                    for dc in range(2):
                        S.mm(ps[:], KT[:, h * 2 + dc, mc * 128:(mc + 1) * 128], qq[:, h * 2 + dc, :],
                             start=dc == 0, stop=dc == 1)
                    e = eT[(h % 2) * 2 + mc]
                    S.act(e[:], ps[:], AF.Exp)
                    es.append(e)
                return es

            def pv(h, es):
                for s in range(4):
                    po = C.bank()
                    for mc in range(2):
                        S.mm(po[:, 0:257], es[mc][:, s * 128:(s + 1) * 128], V[:, mc, h, :], start=mc == 0, stop=mc == 1)
                    r = rl[rli[0] % 4]
                    rli[0] += 1
                    S.generic("vector", lambda e, r=r, po=po: e.reciprocal(r[:], po[:, 256:257]), reads=[po], writes=[r])
                    if s % 2 == 0:
                        S.act(O[s][:, h * 256:(h + 1) * 256], po[:, 0:256], AF.Copy, scale=r[:])
                    else:
                        S.ts(O[s][:, h * 256:(h + 1) * 256], po[:, 0:256], r[:], None, op0=ALU.mult)
            prev = None
            for h in range(4):
                es = scores(h)
                if prev is not None:
                    pv(*prev)
                prev = (h, es)
                if tb >= 1:
                    stC2(tb - 1, h)
            pv(*prev)

        OT4 = [S.sb(f"m_OT4_{i}", [128, 8, 128], BF16) for i in range(4)]

        def stC1(tb):
            for s in range(4):
                pt = C.bank()
                ptb = pt[:].bitcast(BF16)
                for kc in range(8):
                    S.tr(ptb[:, kc * 128:(kc + 1) * 128], O[s][:, kc * 128:(kc + 1) * 128], C.ident[:])
                S.copy(OT4[s][:], ptb.rearrange("p (k t) -> p k t", k=8), eng="vector" if s % 2 == 0 else "scalar")
            tail_load_x(C, T, x_in, tb * 4)
            tail_load_x(C, T, x_in, tb * 4 + 1)

        def stC2(tb, s):
            ti = tb * 4 + s
            ot = OT4[s]
            ya, yb = C.bank(), C.bank()
            for half, y in enumerate((ya, yb)):
                for kc in range(8):
                    S.mm(y[:], ot[:, kc, :], wo[:, kc, half * 512:(half + 1) * 512], start=kc == 0, stop=kc == 7)
            tail_tile(C, T, ya, yb, ti, x_out, xT_out)
            if s + 2 < 4:
                tail_load_x(C, T, x_in, tb * 4 + s + 2)
            tail_tick(C, T)

        stA(0)
        for tb in range(NTB):
            stB(tb)
            if tb + 1 < NTB:
                stA(tb + 1)
            stC1(tb)
        for s in range(4):
            stC2(NTB - 1, s)
        tail_flush(C, T)


def phase_ffn(C, layer, x_in, xT_in, x_out, xT_out, li):
    S, D = C.S, C.D
    with S.scope():
        T = tail_alloc(C, li, g_eng="gpsimd", d1=2, d2=4, d1b=0)
        wd = S.sb("f_wd", [128, NJ, DM], BF16)
        cwb = S.sb("f_cwb", [128, 2 * NJ, 4], F32)
        dma(S, cwb[:], C.I["ffn_cwb"][layer])
        cr_all = [S.sb(f"f_crall{i}", [128, 2 * NJ, 2], F32) for i in range(2)]
        bc = [S.sb(f"f_bc{i}", [128, 2 * NJ, 2], F32) for i in range(2)]
        btmp = S.sb("f_btmp", [128, 2 * NJ], F32)
        S.memset(cr_all[1][:], 0.0, eng="gpsimd")
        actT = [S.sb(f"f_actT{i}", [128, NJ, TB], BF16) for i in range(2)]
        NWU = 6
        wu = [S.sb(f"f_wu{i}", [128, 8, 256], BF16) for i in range(NWU)]
        NCV = 8
        cv = [S.sb(f"f_cv{i}", [128, TB], F32) for i in range(NCV)]
        sg = [S.sb(f"f_sg{i}", [128, TB], F32) for i in range(2)]
        xTb = [S.sb(f"f_xTb{i}", [128, 8, TB], BF16) for i in range(2)]
        load_xT_blk(C, xTb[0], xT_in, 0)
        n_w = NTB * NJ
        for i_ in range(4):
            dma(S, wu[i_][:], D["ffn_up"][layer][i_], eng="scalar")
        for q in range(2):
            dma(S, wd[:, q * 11:(q + 1) * 11, :],
                D["ffn_down"][layer][q * 11 * 128:(q + 1) * 11 * 128, :].rearrange("(j p) n -> p j n", p=128))

        def block_bias(tb):
            cr = cr_all[(tb + 1) % 2]
            o = bc[tb % 2]
            S.tt(o[:, :, 1], cr[:, :, 1], cwb[:, :, 0], ALU.mult)
            S.tt(btmp[:], cr[:, :, 1], cwb[:, :, 1], ALU.mult)
            S.tt(o[:, :, 0], cr[:, :, 0], cwb[:, :, 0], ALU.mult)
            S.tt(o[:, :, 0], o[:, :, 0], btmp[:], ALU.add)

        def stage1(tb, j):
            xt = xTb[tb % 2]
            wi = tb * NJ + j
            if wi + 4 < n_w:
                dma(S, wu[(wi + 4) % NWU][:], D["ffn_up"][layer][(wi + 4) % NJ], eng="scalar")
            w = wu[wi % NWU]
            for half in range(2):
                ps = C.bank()
                for kc in range(8):
                    S.mm(ps[:], w[:, kc, half * 128:(half + 1) * 128], xt[:, kc, :], start=kc == 0, stop=kc == 7)
                idx = half * NJ + j
                c = cv[(2 * j + half) % NCV]
                S.act(c[:], ps[:], AF.Identity, scale=cwb[:, idx, 2:3], bias=cwb[:, idx, 3:4])
                if tb + 1 < NTB:
                    S.copy(cr_all[tb % 2][:, idx, :], ps[:, TB - 2:TB], eng="scalar")
                S.stt(c[:, 1:TB], ps[:, 0:TB - 1], cwb[:, idx, 1:2], c[:, 1:TB], ALU.mult, ALU.add)
                S.stt(c[:, 2:TB], ps[:, 0:TB - 2], cwb[:, idx, 0:1], c[:, 2:TB], ALU.mult, ALU.add)
                if tb > 0:
                    S.tt(c[:, 0:2], c[:, 0:2], bc[tb % 2][:, idx, :], ALU.add, eng="gpsimd")

        def stage2(tb, j):
            sgt = sg[j % 2]
            S.act(sgt[:], cv[(2 * j + 1) % NCV][:], AF.Silu)
            S.tt(actT[tb % 2][:, j, :], cv[(2 * j) % NCV][:], sgt[:], ALU.mult, eng="gpsimd")

        def down(tb, s):
            at = actT[tb % 2]
            ya, yb = C.bank(), C.bank()
            for half, y in enumerate((ya, yb)):
                for j in range(NJ):
                    S.mm(y[:], at[:, j, s * 128:(s + 1) * 128], wd[:, j, half * 512:(half + 1) * 512],
                         start=j == 0, stop=j == NJ - 1)
            tail_tile(C, T, ya, yb, tb * 4 + s, x_out, xT_out)
            if tb * 4 + s + 2 < NT:
                tail_load_x(C, T, x_in, tb * 4 + s + 2)

        tail_load_x(C, T, x_in, 0)
        tail_load_x(C, T, x_in, 1)
        for tb in range(NTB + 1):
            if tb + 1 < NTB:
                load_xT_blk(C, xTb[(tb + 1) % 2], xT_in, tb + 1)
            if 0 < tb < NTB:
                block_bias(tb)
            for j in range(NJ):
                tail_tick(C, T)
                if tb < NTB:
                    stage1(tb, j)
                    if j >= 1:
                        stage2(tb, j - 1)
                if tb >= 1 and j % 5 == 2 and j // 5 < 4:
                    down(tb - 1, j // 5)
            if tb < NTB:
                stage2(tb, NJ - 1)
        tail_flush(C, T)


def phase_sgu(C, x_in, xT_in, x_out, xT_out, li):
    S, D, I = C.S, C.D, C.I
    with S.scope():
        T = tail_alloc(C, li, d1b=0, norm_eng="vector")
        win = load_w(C, "g_win", D["sg_w_in"], 8, 2 * DM)
        wout = load_w(C, "g_wout", D["sg_w_out"], 8, DM, eng="scalar")
        wsf = S.sb("g_wsf", [128, 8, 128], F32)
        dma(S, wsf[:], I["sg_w_sT"].rearrange("h s t -> s h t"))
        S.generic("gpsimd", lambda e: e.affine_select(out=wsf[:], in_=wsf[:], pattern=[[0, 8], [1, 128]],
                                                      compare_op=ALU.is_ge, fill=0.0, base=0, channel_multiplier=-1),
                  reads=[wsf], writes=[wsf])
        wsT = S.sb("g_wsT", [128, 8, 128], BF16)
        S.copy(wsT[:], wsf[:], eng="gpsimd")
        bsT = S.sb("g_bsT", [128, 8], F32)
        dma(S, bsT[:], I["sg_b_sT"])
        ng = S.sb("g_ng", [128, DM], F32)
        nb = S.sb("g_nb", [128, DM], F32)
        dma(S, ng[:], I["sg_norm_g"].partition_broadcast(128))
        dma(S, nb[:], I["sg_norm_b"].partition_broadcast(128))
        xTb = [S.sb(f"g_xTb{i}", [128, 8, TB], BF16) for i in range(2)]
        u = [S.sb(f"g_u{i}", [128, DM], F32) for i in range(3)]
        v = [S.sb(f"g_v{i}", [128, DM], F32) for i in range(2)]
        vb = [S.sb(f"g_vb{i}", [128, DM], BF16) for i in range(2)]
        sgo = [S.sb(f"g_sgo{i}", [128, DM], BF16) for i in range(2)]
        sgT = [S.sb(f"g_sgT{i}", [128, 8, 128], BF16) for i in range(2)]
        st = [S.sb(f"g_st{i}", [128, 12], F32) for i in range(2)]
        mv = [S.sb(f"g_mv{i}", [128, 4], F32) for i in range(2)]
        load_xT_blk(C, xTb[0], xT_in, 0)
        tail_load_x(C, T, x_in, 0)

        def g1(ti):
            tb, s = ti // 4, ti % 4
            k = ti % 2
            if s == 0 and tb + 1 < NTB:
                load_xT_blk(C, xTb[(tb + 1) % 2], xT_in, tb + 1)
            xt = xTb[tb % 2]
            pq = []
            for q in range(4):
                ps = C.bank()
                for kc in range(8):
                    S.mm(ps[:], xt[:, kc, s * 128:(s + 1) * 128], win[:, kc, q * 512:(q + 1) * 512],
                         start=kc == 0, stop=kc == 7)
                pq.append(ps)
            S.act(v[k][:, 0:512], pq[2][:], AF.Gelu_apprx_tanh)
            S.act(v[k][:, 512:1024], pq[3][:], AF.Gelu_apprx_tanh)
            S.act(u[ti % 3][:, 0:512], pq[0][:], AF.Gelu_apprx_tanh)
            S.act(u[ti % 3][:, 512:1024], pq[1][:], AF.Gelu_apprx_tanh)
            vv, stt_, mvv = v[k], st[k], mv[k]
            S.generic("vector", lambda e, vv=vv, stt_=stt_: e.bn_stats(out=stt_[:, 0:6], in_=vv[:, 0:512]), reads=[vv], writes=[stt_])
            S.generic("vector", lambda e, vv=vv, stt_=stt_: e.bn_stats(out=stt_[:, 6:12], in_=vv[:, 512:1024]), reads=[vv], writes=[stt_])
            S.generic("vector", lambda e, mvv=mvv, stt_=stt_: e.bn_aggr(out=mvv[:, 0:2], in_=stt_[:]), reads=[stt_], writes=[mvv])
            S.ts(mvv[:, 2:3], mvv[:, 1:2], LN_EPS_C, None, op0=ALU.add)
            S.tt(mvv[:, 2:3], mvv[:, 2:3], C.neghalf[:], ALU.pow, eng="gpsimd")
            S.ts(mvv[:, 3:4], mvv[:, 0:1], mvv[:, 2:3], -1.0, op0=ALU.mult, op1=ALU.mult)

        def g1b(ti):
            k = ti % 2
            vv, mvv = v[k], mv[k]
            S.act(vv[:], vv[:], AF.Identity, bias=mvv[:, 3:4], scale=mvv[:, 2:3])
            S.tt(vv[:], vv[:], ng[:], ALU.mult, eng="vector")
            S.tt(vb[k][:], vv[:], nb[:], ALU.add, eng="gpsimd")

        def g2(ti):
            k = ti % 2
            p0, p1 = C.bank(), C.bank()
            for h in range(8):
                pp = (p0, p1)[h // 4]
                S.mm(pp[:, (h % 4) * 128:(h % 4 + 1) * 128], wsT[:, h, :], vb[k][:, h * 128:(h + 1) * 128],
                     start=True, stop=True)
            for h in range(8):
                pp = (p0, p1)[h // 4]
                S.stt(sgo[k][:, h * 128:(h + 1) * 128], pp[:, (h % 4) * 128:(h % 4 + 1) * 128], bsT[:, h:h + 1],
                      u[ti % 3][:, h * 128:(h + 1) * 128], ALU.add, ALU.mult)

        def g3(ti):
            k = ti % 2
            pt = C.bank()
            ptb = pt[:].bitcast(BF16)
            for kc in range(8):
                S.tr(ptb[:, kc * 128:(kc + 1) * 128], sgo[k][:, kc * 128:(kc + 1) * 128], C.ident[:])
            S.copy(sgT[k][:], ptb.rearrange("p (k t) -> p k t", k=8), eng="scalar")

        def g4(ti):
            k = ti % 2
            ya, yb = C.bank(), C.bank()
            for half, y in enumerate((ya, yb)):
                for kc in range(8):
                    S.mm(y[:], sgT[k][:, kc, :], wout[:, kc, half * 512:(half + 1) * 512], start=kc == 0, stop=kc == 7)
            tail_tile(C, T, ya, yb, ti, x_out, xT_out)
            if ti + 2 < NT:
                tail_load_x(C, T, x_in, ti + 2)
            tail_tick(C, T)

        tail_load_x(C, T, x_in, 1)
        for kk in range(NT + 4):
            if kk < NT:
                g1(kk)
            if 0 <= kk - 1 < NT:
                g1b(kk - 1)
            if 0 <= kk - 2 < NT:
                g2(kk - 2)
            if 0 <= kk - 3 < NT:
                g3(kk - 3)
            if 0 <= kk - 4 < NT:
                g4(kk - 4)
        tail_flush(C, T)


NSA_STOP = [0]


def phase_nsa(C, x_in, xT_in, x_out, xT_out, li):
    S, D, I = C.S, C.D, C.I
    FM = D["fm"]
    with S.scope():
        win = load_w(C, "a_win", D["ab_w_in"], 8, W_IN_COLS)
        pw = S.sb("a_pw", [128, 4, 128], BF16)
        dma(S, pw[:], D["pool_w"].rearrange("g c d -> c g d"))
        pscale = S.sb("a_pscale", [128, 4], F32)
        dma(S, pscale[:], I["pool_scaleT"])
        io_i = S.sb("a_ioi", [128, 16], I32)
        S.generic("gpsimd", lambda e: e.iota(io_i[:], pattern=[[1, 16]], base=1, channel_multiplier=0), writes=[io_i])
        io_f = S.sb("a_iof", [128, 16], F32)
        S.copy(io_f[:], io_i[:], eng="vector")
        rcnt = S.sb("a_rcnt", [128, 4, 16], F32)
        for gi in range(4):
            S.ts(rcnt[:, gi, :], io_f[:], float(2 ** (gi + 1)), None, op0=ALU.min)
            S.generic("vector", lambda e, gi=gi: e.reciprocal(rcnt[:, gi, :], rcnt[:, gi, :]), reads=[rcnt], writes=[rcnt])
        fm_cols = [(h * 128, 0.125) for h in range(4)]
        fm_cols += [(768, 1.0), (1024, 1.0), (512, 1.0), (640, 1.0)]
        xTb = [S.sb(f"a_xTb{i}", [128, 8, TB], BF16) for i in range(2)]
        fo = [S.sb(f"a_fo{i}", [128, 8, TB], BF16) for i in range(2)]
        ub = [[S.sb(f"a_ub{gi}_{i}", [128, 16 + TB], F32) for i in range(2)] for gi in range(4)]
        wk = [S.sb(f"a_wk{i}", [128, 16 + TB], F32) for i in range(2)]
        pl = [S.sb(f"a_pl{i}", [128, TB], BF16) for i in range(4)]
        ptmp = S.sb("a_ptmp", [128, 16], F32)
        mo = [S.sb(f"a_mo{i}", [128, 4, TB], BF16) for i in range(2)]
        vt = [S.sb(f"a_vt{i}", [128, 4, 65], BF16) for i in range(2)]
        gtt = [S.sb(f"a_gt{i}", [128, 24], F32) for i in range(2)]
        for i in range(2):
            S.memset(vt[i][:, :, 64:65], 1.0, eng="gpsimd")
        for gi in range(4):
            S.memset(ub[gi][1][:, TB:TB + 16], 0.0, eng="gpsimd")
        for i in range(2):
            S.memset(wk[i][:], 0.0, eng="gpsimd")
        load_xT_blk(C, xTb[0], xT_in, 0)
        for tb in range(NTB):
            if tb + 1 < NTB:
                load_xT_blk(C, xTb[(tb + 1) % 2], xT_in, tb + 1)
            xt = xTb[tb % 2]
            f = fo[tb % 2]
            for gi in range(4):
                ps = C.bank()
                c0 = 1304 + gi * 128
                for kc in range(8):
                    S.mm(ps[:], win[:, kc, c0:c0 + 128], xt[:, kc, :], start=kc == 0, stop=kc == 7)
                u = ub[gi][tb % 2]
                up = ub[gi][(tb + 1) % 2]
                S.copy(u[:, 0:16], up[:, TB:TB + 16], eng="gpsimd")
                S.act(u[:, 16:16 + TB], ps[:], AF.Copy)
                cur = u
                W = 16 + TB
                for k in range(gi + 1):
                    sh = 2 ** k
                    nxt = wk[k % 2]
                    S.tt(nxt[:, sh:W], cur[:, sh:W], cur[:, 0:W - sh], ALU.add)
                    cur = nxt
                p = pl[gi]
                S.stt(p[:], cur[:, 16:W], 1.0 / (2 ** (gi + 1)), u[:, 16:W], ALU.mult, ALU.subtract)
                if tb == 0:
                    S.tt(ptmp[:], cur[:, 16:32], rcnt[:, gi, :], ALU.mult)
                    S.tt(p[:, 0:16], ptmp[:], u[:, 16:32], ALU.subtract)
            for gi_, (c0, sc) in enumerate(fm_cols):
                ps = C.bank()
                for kc in range(8):
                    S.mm(ps[:], win[:, kc, c0:c0 + 128], xt[:, kc, :], start=kc == 0, stop=kc == 7)
                if True:
                    S.act(f[:, gi_, :], ps[:], AF.Copy, scale=sc)
                else:
                    S.ts(f[:, gi_, :], ps[:], sc, None, op0=ALU.mult)
            for two in range(2):
                dma(S, FM[:, :, tb * TB:(tb + 1) * TB].rearrange("(pr two) p t -> two p pr t", two=2)[two],
                    f[two * 64:(two + 1) * 64, :, :])
            m = mo[tb % 2]
            for gi in range(4):
                py = C.bank()
                S.mm(py[:], pw[:, gi, :], pl[gi][:], start=True, stop=True)
                S.ts(m[:, gi, :], py[:], pscale[:, gi:gi + 1], None, op0=ALU.mult)
            dma(S, D["mixT"][512:1024, tb * TB:(tb + 1) * TB].rearrange("(g p) t -> p g t", p=128), m[:])
            for s in range(4):
                ti = tb * 4 + s
                ps = C.bank()
                for kc in range(8):
                    S.mm(ps[:, 0:128], xt[:, kc, s * 128:(s + 1) * 128], win[:, kc, 896:1024], start=kc == 0, stop=kc == 7)
                for kc in range(8):
                    S.mm(ps[:, 128:280], xt[:, kc, s * 128:(s + 1) * 128], win[:, kc, 1152:1304], start=kc == 0, stop=kc == 7)
                v = vt[ti % 2]
                gt = gtt[ti % 2]
                S.copy(v[:, :, 0:64], ps[:, 0:256].rearrange("p (a d) -> p a d", a=4), eng="vector")
                S.copy(gt[:], ps[:, 256:280], eng="vector")
                dma(S, D["vt"][ti], v[:].rearrange("p a d -> p (a d)"))
                dma(S, D["gt"][ti], gt[:])

    if NSA_STOP[0] == 1:
        return
    with S.scope():
        pe1 = S.sb("b_pe1", [64, 2, 32], BF16)
        dma(S, pe1[:], D["cmp_peT"].rearrange("k d l -> d k l"))
        w1 = S.sb("b_w1", [64, 2, 32, 128], BF16)
        for kvi in range(2):
            dma(S, w1[:, kvi, :, :], D["cmp_w1"][kvi].rearrange("(l d) h -> d l h", d=64), eng="scalar")
        c0 = S.sb("b_c0", [64, 4, SEQ], BF16)
        for q in range(4):
            dma(S, c0[:, q, :], FM[12 + q])
        pe2 = S.sb("b_pe2", [64, 2, 32, 2], BF16)
        for kvi in range(2):
            for dd in range(2):
                S.copy(pe2[:, kvi, :, dd], pe1[:, kvi, :], eng="vector")
        w2 = S.sb("b_w2", [128, 2, 64], BF16)
        dma(S, w2[:], D["cmp_w2"].rearrange("k h d -> h k d"))
        cb = S.sb("b_cb", [128, 2], F32)
        for kvi in range(2):
            pb = C.bank()
            for l in range(32):
                S.mm(pb[:, 0:2], w1[:, kvi, l, :], pe2[:, kvi, l, :], start=l == 0, stop=l == 31)
            S.copy(cb[:, kvi:kvi + 1], pb[:, 0:1], eng="vector")
        hid = [S.sb(f"b_hid{i}", [128, 256], BF16) for i in range(4)]
        kcs = S.sb("b_kcs", [64, 2, 256], BF16)
        vcs = S.sb("b_vcs", [128, 4, 65], BF16)
        S.memset(vcs[:, :, 64:65], 1.0, eng="gpsimd")
        for kvi in range(2):
            for g in range(2):
                ph = C.bank()
                for l in range(32):
                    S.mm(ph[:, 0:255], w1[:, kvi, l, :], c0[:, kvi * 2 + g, l:l + 4065:16], start=l == 0, stop=l == 31)
                hd = hid[kvi * 2 + g]
                S.memset(hd[:, 255:256], 0.0, eng="gpsimd")
                S.act(hd[:, 0:255], ph[:, 0:255], AF.Gelu_apprx_tanh, bias=cb[:, kvi:kvi + 1])
                if kvi == 0:
                    pk = C.bank()
                    S.mm(pk[0:64, 0:256], w2[:, 0, :], hd[:], start=True, stop=True)
                    S.copy(kcs[:, g, :], pk[0:64, 0:256], eng="vector")
                else:
                    for n_ in range(2):
                        pv = C.bank()
                        S.mm(pv[:, 0:64], hd[:, n_ * 128:(n_ + 1) * 128], w2[:, 1, :], start=True, stop=True)
                        S.copy(vcs[:, g * 2 + n_, 0:64], pv[:, 0:64], eng="vector")
        dma(S, D["kcT"].rearrange("g p n -> p g n"), kcs[:])
        dma(S, D["vc"].rearrange("g n p c -> p (g n) c"), vcs[:])

    if NSA_STOP[0] == 2:
        return
    with S.scope():
        kTs = S.sb("d_kTs", [128, 4, SEQ], BF16)
        S.memset(kTs[64:128, :, :], 0.0, eng="vector")
        kc_ = S.sb("c_kc", [128, 2, 256], BF16)
        S.memset(kc_[64:128, :, :], 0.0, eng="gpsimd")
        dma(S, kc_[0:64, :, :], D["kcT"].rearrange("g p n -> p g n"))
        q8 = [S.sb(f"d_q8{i}", [128, 8, TB], BF16) for i in range(2)]
        for i in range(2):
            S.memset(q8[i][64:128, :, :], 0.0, eng="gpsimd")

        def load_q8(dst, tb):
            dma(S, dst[0:64, :, :], FM[0:8, :, tb * TB:(tb + 1) * TB].rearrange("h p t -> p h t"))
        load_q8(q8[0], 0)
        load_q8(q8[1], 1)
        zf = S.sb("c_zf", [128, 504], F32)
        S.memset(zf[:], 0.0, eng="gpsimd")
        S.generic("gpsimd", lambda e: e.affine_select(out=zf[:], in_=zf[:], pattern=[[-16, 504]], compare_op=ALU.is_ge,
                                                      fill=NEGB, base=16 * 248 - 31, channel_multiplier=1),
                  reads=[zf], writes=[zf])
        Tcb = S.sb("c_Tcb", [128, 504], BF16)
        S.copy(Tcb[:], zf[:], eng="gpsimd")
        Tf = S.sb("c_Tf", [128, 128], F32)
        S.memset(Tf[:], 0.0, eng="gpsimd")
        S.memset(Tf[0:64, 63:65], 100.0, eng="gpsimd")
        S.memset(Tf[0:64, 65:128], -1e30, eng="gpsimd")
        S.memset(Tf[64:128, 64:66], 100.0, eng="gpsimd")
        S.memset(Tf[64:128, 66:128], -1e30, eng="gpsimd")
        P = [S.sb(f"c_P{i}", [128, 4, 256], F32) for i in range(2)]
        nm = [S.sb(f"c_nm{i}", [128, 4], F32) for i in range(2)]
        sm = [S.sb(f"c_sm{i}", [128, 4], F32) for i in range(2)]
        imp = [S.sb(f"c_imp{i}", [128, 64], F32) for i in range(2)]
        imp2 = [S.sb(f"c_imp2{i}", [128, 64], F32) for i in range(2)]
        sc_ = [S.sb(f"c_sc{i}", [128, 64], F32) for i in range(2)]
        wk_ = [S.sb(f"c_wk{i}", [128, 64], F32) for i in range(2)]
        m8 = [S.sb(f"c_m8{i}", [128, 16], F32) for i in range(2)]
        selm = [S.sb(f"c_selm{i}", [128, 64], BF16) for i in range(2)]
        selQ = [S.sb(f"d_selQ{i}", [128, 2, TB], BF16) for i in range(NTB)]
        for i in range(NTB):
            S.memset(selQ[i][64:128, :, :], 0.0, eng="gpsimd")
        it = [0]
        a3_pend = []
        C.bank_cls = {"s": [0, 1, 2], "po": [4], "pq": [3, 5], "a3": [6, 7]}

        def a3_flush(keep=0):
            while len(a3_pend) > keep:
                a3_pend.pop(0)()

        def a3_unit(ti, g, q):
            tb, s = ti // 4, ti % 4
            k = it[0] % 2
            it[0] += 1
            pS = [C.bank("a3"), C.bank("a3")]
            for r in range(4):
                o = pS[r // 2][:, (r % 2) * 256:(r % 2 + 1) * 256]
                S.mm(o, q[:, 4 * g + r, s * 128:(s + 1) * 128], kc_[:, g, :], start=True, stop=False)
                S.mm(o, C.ident[:], Tcb[:, 248 - 8 * ti:504 - 8 * ti], start=False, stop=True)
            Pk, nmk, smk = P[k], nm[k], sm[k]
            for r in range(4):
                o = pS[r // 2][:, (r % 2) * 256:(r % 2 + 1) * 256]
                S.act(Pk[:, r, :], o, AF.Exp, accum_out=smk[:, r:r + 1])
            S.ts(smk[:], smk[:], 1e-30, None, op0=ALU.add)
            S.generic("vector", lambda e, smk=smk: e.reciprocal(smk[:], smk[:]), reads=[smk], writes=[smk])
            for r in range(4):
                S.ts(Pk[:, r, :], Pk[:, r, :], smk[:, r:r + 1], None, op0=ALU.mult)
            ik, i2k, sk, wkk, mk, slk = imp[k], imp2[k], sc_[k], wk_[k], m8[k], selm[k]
            S.generic("vector", lambda e, ik=ik, Pk=Pk: e.tensor_reduce(
                out=ik[:], in_=Pk[:].rearrange("p r (j k) -> p j r k", k=4), axis=AX.XY, op=ALU.add),
                reads=[Pk], writes=[ik])
            S.generic("vector", lambda e, i2k=i2k, Pk=Pk: e.tensor_reduce(
                out=i2k[:], in_=Pk[:, :, 3:256:4].rearrange("p r j -> p j r"), axis=AX.X, op=ALU.add),
                reads=[Pk], writes=[i2k])
            S.tt(ik[:, 1:64], ik[:, 1:64], i2k[:, 0:63], ALU.add)
            S.tt(sk[:], ik[:], Tf[:, 64 - 2 * ti:128 - 2 * ti], ALU.add)
            S.ts(sk[:, 0:1], sk[:, 0:1], 100.0, None, op0=ALU.add)
            S.generic("vector", lambda e, mk=mk, sk=sk: e.max(out=mk[:, 0:8], in_=sk[:]), reads=[sk], writes=[mk])
            S.generic("vector", lambda e, mk=mk, sk=sk, wkk=wkk: e.match_replace(
                out=wkk[:], in_to_replace=mk[:, 0:8], in_values=sk[:], imm_value=-1e30),
                reads=[sk, mk], writes=[wkk])
            S.generic("vector", lambda e, mk=mk, wkk=wkk: e.max(out=mk[:, 8:16], in_=wkk[:]), reads=[wkk], writes=[mk])
            S.ts(slk[:], sk[:], mk[:, 15:16], -1.0, op0=ALU.is_ge, op1=ALU.add)

            def fin(slk=slk, tb=tb, g=g, s=s):
                pt = C.bank("pq")
                ptb = pt[:].bitcast(BF16)
                S.tr(ptb[0:64, 0:128], slk[:], C.ident[:])
                S.copy(selQ[tb][0:64, g, s * 128:(s + 1) * 128], ptb[0:64, 0:128], eng="vector")
            a3_pend.append(fin)

        vcs = S.sb("d_vcs", [128, 4, 65], BF16)
        dma(S, vcs[:], D["vc"].rearrange("g n p c -> p (g n) c"))
        vts = S.sb("d_vts", [128, NT, 260], BF16)
        dma(S, vts[:], D["vt"].rearrange("i p c -> p i c"))
        gts = S.sb("d_gts", [128, NT, 24], F32)
        dma(S, gts[:], D["gt"].rearrange("i p c -> p i c"))
        for q_ in range(4):
            dma(S, kTs[0:64, q_, :], FM[8 + q_])
        zb = S.sb("d_zb", [128, TB], BF16)
        S.memset(zb[:], 0.0, eng="gpsimd")
        caus = S.sb("d_caus", [128, 4, TB], BF16)
        band = S.sb("d_band", [128, 4, TB], BF16)
        for d_ in range(4):
            S.generic("gpsimd", lambda e, d_=d_: e.affine_select(out=caus[:, d_, :], in_=zb[:], pattern=[[1, TB]],
                                                                 compare_op=ALU.is_ge, fill=NEGB, base=-128 * d_,
                                                                 channel_multiplier=-1), reads=[zb], writes=[caus])
            S.generic("gpsimd", lambda e, d_=d_: e.affine_select(out=band[:, d_, :], in_=zb[:], pattern=[[-1, TB]],
                                                                 compare_op=ALU.is_ge, fill=NEGB, base=128 * d_ - 1,
                                                                 channel_multiplier=1), reads=[zb], writes=[band])
        cmpb = {}
        for (qb, c) in [(q_, 0) for q_ in range(5)] + [(q_, 1) for q_ in range(4, 8)]:
            t_ = S.sb(f"d_cmpb{qb}_{c}", [128, TB], BF16)
            S.generic("gpsimd", lambda e, t_=t_, qb=qb, c=c: e.affine_select(
                out=t_[:], in_=zb[:], pattern=[[1, TB]], compare_op=ALU.is_ge, fill=NEGB,
                base=512 * qb - 2048 * c - 31, channel_multiplier=-16), reads=[zb], writes=[t_])
            cmpb[(qb, c)] = t_
        Wide = S.sb("d_Wide", [128, SEQ], BF16)
        S.memset(Wide[64:128, :], 0.0, eng="gpsimd")
        S.memset(Wide[0:64, :], -NEGB, eng="gpsimd")
        S.generic("gpsimd", lambda e: e.affine_select(out=Wide[0:64, :], in_=Wide[0:64, :], pattern=[[1, SEQ]], compare_op=ALU.is_ge,
                                                      fill=0.0, base=0, channel_multiplier=-64), reads=[Wide], writes=[Wide])
        S.generic("gpsimd", lambda e: e.affine_select(out=Wide[0:64, :], in_=Wide[0:64, :], pattern=[[-1, SEQ]], compare_op=ALU.is_ge,
                                                      fill=0.0, base=63, channel_multiplier=64), reads=[Wide], writes=[Wide])
        NE = 42
        epool = [S.sb(f"d_e{i}", [128, TB], BF16) for i in range(NE)]
        ei = 0
        mx = [[S.sb(f"d_mx{i}_{s}", [128, 512], BF16) for s in range(4)] for i in range(2)]
        acc = [S.sb(f"d_acc{i}", [128, 64], F32) for i in range(8)]
        rl4 = [S.sb(f"d_rl{i}", [128, 4], F32) for i in range(4)]
        mT = [S.sb(f"d_mT{i}", [128, 4, TB], BF16) for i in range(2)]

        S.act(gts[:], gts[:], AF.Sigmoid)
        ri = [0]

        oT_sb = [S.sb(f"d_oT{i}", [65, TB], F32) for i in range(4)]
        oi = [0]

        def pv_unit(qb, h, br, es):
            po = C.bank("po")
            n = len(es)
            for ui, (c, e, v, lo_, hi_) in enumerate(es):
                S.mm(po[0:65, lo_:hi_], v, e[:, lo_:hi_], start=ui == 0, stop=ui == n - 1, skip_group_check=True)
            osb = oT_sb[oi[0] % 4]
            oi[0] += 1
            S.copy(osb[:], po[0:65, :], eng="scalar" if qb < 4 else "vector")
            return (qb, h, br, osb)

        def fin_unit(qb, h, br, osb):
            po = C.bank("pq")
            for s in range(4):
                S.tr(po[:, s * 65:(s + 1) * 65], osb[:, s * 128:(s + 1) * 128], C.ident_f[0:65, 0:65])
            r4 = rl4[ri[0] % 4]
            ri[0] += 1
            S.ts(r4[:], po[:, 64:260:65], 1e-30, None, op0=ALU.add)
            S.generic("vector", lambda e, r4=r4: e.reciprocal(r4[:], r4[:]), reads=[r4], writes=[r4])
            S.tt(r4[:], r4[:], gts[:, 4 * qb:4 * qb + 4, h * 3 + br], ALU.mult)
            for s in range(4):
                a = acc[(h % 2) * 4 + s]
                dst = mx[qb % 2][s][:, h * 64:(h + 1) * 64] if br == 2 else a[:]
                if br == 0:
                    S.ts(dst, po[:, s * 65:s * 65 + 64], r4[:, s:s + 1], None, op0=ALU.mult)
                else:
                    S.stt(dst, po[:, s * 65:s * 65 + 64], r4[:, s:s + 1], a[:], ALU.mult, ALU.add)

        for qb in range(NTB):
            if 1 <= qb and qb + 1 < NTB:
                load_q8(q8[(qb + 1) % 2], qb + 1)
            q = q8[qb % 2]
            pend = None
            fin_q = []
            for h in range(8):
                g = h // 4
                for br in range(3):
                    chunks = []
                    if br == 0:
                        for c in range(2):
                            if c == 1 and qb < 4:
                                continue
                            bl = []
                            if (qb, c) in cmpb:
                                bl.append((C.ident[:], cmpb[(qb, c)][:]))
                            chunks.append((c, kc_[:, g, c * 128:(c + 1) * 128], vcs[:, g * 2 + c, :], bl, 0, TB))
                    elif br == 1:
                        for c in range(4 * qb + 4):
                            bl = []
                            if qb >= 2:
                                bl.append((Wide[:, c * 128:(c + 1) * 128], selQ[qb][:, g, :]))
                            lo_ = 0
                            if c >= 4 * qb:
                                bl.append((C.ident[:], caus[:, c - 4 * qb, :]))
                                lo_ = (c - 4 * qb) * 128
                            chunks.append((c, kTs[:, g, c * 128:(c + 1) * 128], vts[:, c, g * 65:(g + 1) * 65], bl, lo_, TB))
                    else:
                        for c in range(max(0, 4 * qb - 4), 4 * qb + 4):
                            lo_, hi_ = 0, TB
                            if c >= 4 * qb:
                                bl = [(C.ident[:], caus[:, c - 4 * qb, :])]
                                lo_ = (c - 4 * qb) * 128
                            else:
                                bl = [(C.ident[:], band[:, c - (4 * qb - 4), :])]
                                hi_ = (c - (4 * qb - 4) + 1) * 128
                            chunks.append((c, kTs[:, 2 + g, c * 128:(c + 1) * 128], vts[:, c, (2 + g) * 65:(3 + g) * 65], bl, lo_, hi_))
                    es = []
                    for (c, kT, v, bl, lo_, hi_) in chunks:
                        ps = C.bank("s")
                        S.mm(ps[:, lo_:hi_], kT, q[:, h, lo_:hi_], start=True, stop=len(bl) == 0)
                        for bi, (l_, r_) in enumerate(bl):
                            S.mm(ps[:, lo_:hi_], l_, r_[:, lo_:hi_], start=False, stop=bi == len(bl) - 1)
                        e = epool[ei % NE]
                        ei += 1
                        S.act(e[:, lo_:hi_], ps[:, lo_:hi_], AF.Exp)
                        es.append((c, e, v, lo_, hi_))
                    if len(fin_q) >= 2:
                        fin_unit(*fin_q.pop(0))
                    if pend is not None:
                        fin_q.append(pv_unit(*pend))
                    pend = (qb, h, br, es)
                C.drip()
                a3_flush()
                if 2 <= qb + 1 < NTB:
                    a3_unit((qb + 1) * 4 + h // 2, h % 2, q8[(qb + 1) % 2])
            fin_q.append(pv_unit(*pend))
            while fin_q:
                fin_unit(*fin_q.pop(0))
            a3_flush()
            m_ = mT[qb % 2]
            for s in range(4):
                pt = C.bank("pq")
                ptb = pt[:].bitcast(BF16)
                for kc in range(4):
                    S.tr(ptb[:, kc * 128:(kc + 1) * 128], mx[qb % 2][s][:, kc * 128:(kc + 1) * 128], C.ident[:])
                S.copy(m_[:, :, s * 128:(s + 1) * 128], ptb[:, 0:512].rearrange("p (k t) -> p k t", k=4), eng="vector")
            dma(S, D["mixT"][0:512, qb * TB:(qb + 1) * TB].rearrange("(kc p) t -> p kc t", p=128), m_[:])
        C.bank_cls = None

    if NSA_STOP[0] == 4:
        return
    with S.scope():
        T = tail_alloc(C, li, d1b=0)
        wout = load_w(C, "e_wout", D["ab_w_out"], 8, DM)
        mb = [S.sb(f"e_mb{i}", [128, 8, TB], BF16) for i in range(2)]
        load_xT_blk(C, mb[0], D["mixT"], 0)
        tail_load_x(C, T, x_in, 0)
        tail_load_x(C, T, x_in, 1)
        for tb in range(NTB):
            if tb + 1 < NTB:
                load_xT_blk(C, mb[(tb + 1) % 2], D["mixT"], tb + 1)
            for s in range(4):
                ti = tb * 4 + s
                ya, yb = C.bank(), C.bank()
                for half, y in enumerate((ya, yb)):
                    for kc in range(8):
                        S.mm(y[:], mb[tb % 2][:, kc, s * 128:(s + 1) * 128], wout[:, kc, half * 512:(half + 1) * 512],
                             start=kc == 0, stop=kc == 7)
                tail_tile(C, T, ya, yb, ti, x_out, xT_out)
                if ti + 2 < NT:
                    tail_load_x(C, T, x_in, ti + 2)
                tail_tick(C, T)
        tail_flush(C, T)


PHASES_ALL = ["A", "M0", "F0", "G", "M1", "F1"]
_CACHE = {}


def host_inputs(inp, b):
    f = np.float32
    c = np.ascontiguousarray
    m = {}
    m["x"] = c(inp["x"][b], dtype=f)
    m["xT"] = c(inp["x"][b].T, dtype=f)
    m["memT"] = c(inp["mem"][b].T, dtype=f)
    m["ab_w_in"] = c(inp["ab_w_in"][0], dtype=f)
    m["cmp_peT"] = c(np.transpose(inp["nsa_cmp_pe"][0], (0, 2, 1)), dtype=f)
    m["cmp_w1"] = c(inp["nsa_cmp_w1"][0], dtype=f)
    m["cmp_w2"] = c(inp["nsa_cmp_w2"][0], dtype=f)
    m["pool_w"] = c(inp["pool_w"][0], dtype=f)
    m["pool_scaleT"] = c(inp["pool_scale"][0].reshape(4, 128).T, dtype=f)
    m["ab_w_out"] = c(inp["ab_w_out"][0], dtype=f)
    m["sg_w_in"] = c(inp["sg_w_in"][0], dtype=f)
    m["sg_norm_g"] = c(inp["sg_norm_g"][0:1], dtype=f)
    m["sg_norm_b"] = c(inp["sg_norm_b"][0:1], dtype=f)
    m["sg_w_sT"] = c(np.transpose(inp["sg_w_s"][0], (0, 2, 1)), dtype=f)
    m["sg_b_sT"] = c(inp["sg_b_s"][0].T, dtype=f)
    m["sg_w_out"] = c(inp["sg_w_out"][0], dtype=f)
    m["ln_g"] = c(inp["ln_g"].reshape(6, DM), dtype=f)
    m["ln_b"] = c(inp["ln_b"].reshape(6, DM), dtype=f)
    m["mem_wq"] = c(inp["mem_wq"], dtype=f)
    m["mem_wkv"] = c(inp["mem_wkv"], dtype=f)
    m["mem_wo"] = c(inp["mem_wo"], dtype=f)
    m["ffn_w_up"] = c(inp["ffn_w_up"], dtype=f)
    cw = np.concatenate([np.transpose(inp["ffn_conv_w"], (0, 2, 1)), inp["ffn_conv_b"][:, :, None]], axis=2)
    m["ffn_cwb"] = c(np.transpose(cw.reshape(2, 2 * NJ, 128, 4), (0, 2, 1, 3)), dtype=f)
    m["ffn_w_down"] = c(inp["ffn_w_down"], dtype=f)
    return m


def kernel(**inputs):
    inp = {k: np.asarray(v) for k, v in inputs.items()}
    if "nc" not in _CACHE:
        nc = bass.Bass("TRN2", target_bir_lowering=False)
        build_program(nc, PHASES_ALL)
        _CACHE["nc"] = nc
    nc = _CACHE["nc"]
    shared = host_inputs(inp, 0)
    in_maps = []
    for b in range(8):
        m = dict(shared)
        m["x"] = np.ascontiguousarray(inp["x"][b], dtype=np.float32)
        m["xT"] = np.ascontiguousarray(inp["x"][b].T, dtype=np.float32)
        m["memT"] = np.ascontiguousarray(inp["mem"][b].T, dtype=np.float32)
        in_maps.append(m)
    res = run_bass_kernel_spmd(nc, in_maps, core_ids=list(range(8)))
    return np.stack([np.asarray(r["out"], dtype=np.float32) for r in res.results], axis=0)
```

```python
import numpy as np

from concourse.bass_utils import run_bass_kernel_spmd

import contextlib
import concourse.bass as bass
import concourse.mybir as mybir

F32 = mybir.dt.float32
BF16 = mybir.dt.bfloat16
I32 = mybir.dt.int32
AF = mybir.ActivationFunctionType
ALU = mybir.AluOpType
AX = mybir.AxisListType

ENGS = ("tensor", "vector", "scalar", "gpsimd", "sync")
SEM_CHUNK = 12000
N_DMA_SEMS = 12


def _key(ap):
    if isinstance(ap, (str, tuple)):
        return ap
    t = getattr(ap, "tensor", None)
    return t.name if t is not None else ap.name


class Op:
    __slots__ = ("eng", "fn", "deps", "is_dma", "idx", "eidx", "milestone",
                 "sem", "val", "pe_mm", "dma_prev")

    def __init__(self, eng, fn, is_dma, pe_mm):
        self.eng = eng
        self.fn = fn
        self.deps = set()
        self.is_dma = is_dma
        self.milestone = False
        self.sem = None
        self.val = 0
        self.pe_mm = pe_mm
        self.dma_prev = None


class Sched:
    def __init__(self, nc):
        self.nc = nc
        self.ops = []
        self.last_w = {}
        self.readers = {}
        self.stack = contextlib.ExitStack()
        self.dma_rr = {e: 0 for e in ENGS}
        self.dma_last = {}
        self.n_alloc = 0

    def sb(self, name, shape, dtype):
        self.n_alloc += 1
        return self.stack.enter_context(self.nc.sbuf_tensor(f"{name}_u{self.n_alloc}", list(shape), dtype))

    def ps(self, name, shape, dtype=F32):
        return self.stack.enter_context(self.nc.psum_tensor(name, list(shape), dtype))

    @contextlib.contextmanager
    def scope(self):
        outer = self.stack
        inner = contextlib.ExitStack()
        self.stack = inner
        try:
            yield
        finally:
            self.barrier()
            inner.close()
            self.stack = outer

    def add(self, eng, fn, reads=(), writes=(), is_dma=False, pe_mm=False):
        op = Op(eng, fn, is_dma, pe_mm)
        op.idx = len(self.ops)
        rk = [_key(r) for r in reads if r is not None]
        wk = [_key(w) for w in writes if w is not None]
        for k in rk:
            if isinstance(k, str) and k.startswith("bank") and k not in wk:
                wk.append(k)
        for k in rk:
            lw = self.last_w.get(k)
            if lw is not None:
                op.deps.add(lw)
        for k in wk:
            lw = self.last_w.get(k)
            if lw is not None:
                op.deps.add(lw)
            for r in self.readers.get(k, ()):
                op.deps.add(r)
        for k in rk:
            self.readers.setdefault(k, []).append(op.idx)
        for k in wk:
            self.last_w[k] = op.idx
            self.readers[k] = []
        op.deps.discard(op.idx)
        if is_dma:
            slot = self.dma_rr[eng] % N_DMA_SEMS
            self.dma_rr[eng] += 1
            prev = self.dma_last.get((eng, slot))
            op.sem = (eng, slot)
            op.val = (prev.val if prev is not None else 0) + 16
            op.dma_prev = prev
            self.dma_last[(eng, slot)] = op
        self.ops.append(op)
        return op

    def barrier(self):
        last = {}
        for op in self.ops:
            if op.fn is None:
                continue
            if op.is_dma:
                last[("dma",) + op.sem] = op.idx
            else:
                last[op.eng] = op.idx
        deps = set(last.values())
        for e in ENGS:
            op = self.add(e, None)
            op.deps |= deps
        self.last_w = {}
        self.readers = {}

    def emit(self):
        nc = self.nc
        ops = self.ops
        for op in ops:
            nd = set()
            for d in op.deps:
                dop = ops[d]
                assert dop.fn is not None
                if (not dop.is_dma) and dop.eng == op.eng and op.eng == "tensor":
                    continue
                nd.add(d)
            op.deps = nd
        for op in ops:
            for d in op.deps:
                if not ops[d].is_dma:
                    ops[d].milestone = True
        cnt = {e: 0 for e in ENGS}
        for op in ops:
            if op.is_dma or op.fn is None:
                continue
            if op.milestone:
                c = cnt[op.eng]
                op.sem = (op.eng, "c", c // SEM_CHUNK)
                op.val = c % SEM_CHUNK + 1
                cnt[op.eng] = c + 1
        sem_names = set()
        for op in ops:
            if op.sem is not None and (op.is_dma or op.milestone):
                sem_names.add(op.sem)
        sems = {}
        for sn in sorted(sem_names, key=str):
            sems[sn] = self.stack.enter_context(
                nc.semaphore("s_" + "_".join(str(x) for x in sn)))
        self.n_sems = len(sems)
        per_eng = {e: [op for op in ops if op.eng == e] for e in ENGS}
        stats = {"waits": 0, "insts": 0}

        def run(e, eng):
            waited = {}

            def wait(sn, val):
                if waited.get(sn, 0) >= val:
                    return
                eng.wait_ge(sems[sn], val)
                waited[sn] = val
                stats["waits"] += 1

            for op in per_eng[e]:
                for d in sorted(op.deps):
                    dop = ops[d]
                    wait(dop.sem, dop.val)
                if op.is_dma and op.dma_prev is not None:
                    wait(op.sem, op.dma_prev.val)
                if op.fn is None:
                    continue
                inst = op.fn(eng)
                stats["insts"] += 1
                if op.is_dma:
                    inst.then_inc(sems[op.sem], 16)
                elif op.milestone:
                    inst.then_inc(sems[op.sem], 1)
            for (de, slot), lop in self.dma_last.items():
                if de == e:
                    wait(lop.sem, lop.val)

        with nc.Block() as block:
            @block.sync
            def _(eng):
                run("sync", eng)

            @block.scalar
            def _(eng):
                run("scalar", eng)

            @block.gpsimd
            def _(eng):
                run("gpsimd", eng)

            @block.vector
            def _(eng):
                run("vector", eng)

            @block.tensor
            def _(eng):
                run("tensor", eng)
        self.stats = stats
        self.stack.close()

    def dma(self, out, in_, eng="sync", extra_r=(), extra_w=(), **kw):
        return self.add(eng, lambda e: e.dma_start(out=out, in_=in_, **kw),
                        reads=[in_, *extra_r], writes=[out, *extra_w], is_dma=True)

    def mm(self, out, lhsT, rhs, start=True, stop=True, **kw):
        return self.add("tensor",
                        lambda e: e.matmul(out, lhsT, rhs, start=start, stop=stop, **kw),
                        reads=[lhsT, rhs] + ([] if start else [out]), writes=[out], pe_mm=True)

    def tr(self, out, in_, ident):
        return self.add("tensor", lambda e: e.transpose(out, in_, ident),
                        reads=[in_, ident], writes=[out], pe_mm=True)

    def act(self, out, in_, func, bias=None, scale=None, accum_out=None, eng="scalar"):
        kw = {}
        rd = [in_]
        if bias is not None:
            kw["bias"] = bias
            if not isinstance(bias, (int, float)):
                rd.append(bias)
        if scale is not None:
            kw["scale"] = scale
            if not isinstance(scale, (int, float)):
                rd.append(scale)
        wr = [out]
        if accum_out is not None:
            kw["accum_out"] = accum_out
            wr.append(accum_out)
        return self.add("scalar", lambda e: e.activation(out, in_, func, **kw),
                        reads=rd, writes=wr)

    def tt(self, out, in0, in1, op, eng="vector"):
        return self.add(eng, lambda e: e.tensor_tensor(out, in0, in1, op),
                        reads=[in0, in1], writes=[out])

    def ts(self, out, in0, s1, s2=None, op0=ALU.mult, op1=None, eng="vector", accum_out=None):
        rd = [in0]
        if not isinstance(s1, (int, float)):
            rd.append(s1)
        if s2 is not None and not isinstance(s2, (int, float)):
            rd.append(s2)
        kw = {}
        if op1 is not None:
            kw["op1"] = op1
        wr = [out]
        if accum_out is not None:
            kw["accum_out"] = accum_out
            wr.append(accum_out)
        return self.add(eng, lambda e: e.tensor_scalar(out, in0, s1, s2, op0, **kw),
                        reads=rd, writes=wr)

    def stt(self, out, in0, scalar, in1, op0, op1, eng="vector"):
        rd = [in0, in1]
        if not isinstance(scalar, (int, float)):
            rd.append(scalar)
        return self.add(eng, lambda e: e.scalar_tensor_tensor(out, in0, scalar, in1, op0, op1),
                        reads=rd, writes=[out])

    def copy(self, out, in_, eng="vector"):
        if eng == "scalar":
            return self.add(eng, lambda e: e.copy(out, in_), reads=[in_], writes=[out])
        return self.add(eng, lambda e: e.tensor_copy(out, in_), reads=[in_], writes=[out])

    def memset(self, ap, val, eng="vector"):
        return self.add(eng, lambda e: e.memset(ap, val), writes=[ap])

    def generic(self, eng, fn, reads=(), writes=()):
        return self.add(eng, fn, reads=reads, writes=writes)


SEQ = 4096
DM = 1024
NT = SEQ // 128
TB = 512
NTB = SEQ // TB
DFF = 2816
NJ = DFF // 128
ALPHA_C = 4.0 ** 0.25
LN_EPS_C = 1e-5
NEGB = -30000.0
W_IN_COLS = 1816


class Ctx:
    pass


def _is_dram(ap):
    return str(getattr(ap, "space", "")) == "DRAM" or "DRam" in type(getattr(ap, "tensor", ap)).__name__


def dma(S, out, in_, eng="sync", rkey=None, wkey=None, **kw):
    def dk(ap, k):
        if k is not None:
            return k
        n = ap.tensor.name
        return ("W", n) if n.startswith("s_w_") else None
    r = dk(in_, rkey) if _is_dram(in_) else in_
    w = dk(out, wkey) if _is_dram(out) else out
    return S.add(eng, lambda e: e.dma_start(out=out, in_=in_, **kw),
                 reads=[r], writes=[w], is_dma=True)


def build_program(nc, phases, debug_out=None):
    S = Sched(nc)
    C = Ctx()
    C.S = S
    C.nc = nc

    def din(name, shape):
        return nc.dram_tensor(name, list(shape), F32, kind="ExternalInput").ap()

    def dscr(name, shape, dt=BF16):
        return nc.dram_tensor(name, list(shape), dt, kind="Internal").ap()

    I = {}
    I["x"] = din("x", [SEQ, DM])
    I["xT"] = din("xT", [DM, SEQ])
    I["memT"] = din("memT", [DM, 256])
    I["ab_w_in"] = din("ab_w_in", [DM, W_IN_COLS])
    I["cmp_peT"] = din("cmp_peT", [2, 64, 32])
    I["cmp_w1"] = din("cmp_w1", [2, 2048, 128])
    I["cmp_w2"] = din("cmp_w2", [2, 128, 64])
    I["pool_w"] = din("pool_w", [4, 128, 128])
    I["pool_scaleT"] = din("pool_scaleT", [128, 4])
    I["ab_w_out"] = din("ab_w_out", [DM, DM])
    I["sg_w_in"] = din("sg_w_in", [DM, 2 * DM])
    I["sg_norm_g"] = din("sg_norm_g", [1, DM])
    I["sg_norm_b"] = din("sg_norm_b", [1, DM])
    I["sg_w_sT"] = din("sg_w_sT", [8, 128, 128])
    I["sg_b_sT"] = din("sg_b_sT", [128, 8])
    I["sg_w_out"] = din("sg_w_out", [DM, DM])
    I["ln_g"] = din("ln_g", [6, DM])
    I["ln_b"] = din("ln_b", [6, DM])
    I["mem_wq"] = din("mem_wq", [2, DM, DM])
    I["mem_wkv"] = din("mem_wkv", [2, DM, 2 * DM])
    I["mem_wo"] = din("mem_wo", [2, DM, DM])
    I["ffn_w_up"] = din("ffn_w_up", [2, DM, 2 * DFF])
    I["ffn_cwb"] = din("ffn_cwb", [2, 128, 2 * NJ, 4])
    I["ffn_w_down"] = din("ffn_w_down", [2, DFF, DM])
    out_d = nc.dram_tensor("out", [SEQ, DM], F32, kind="ExternalOutput").ap()
    C.I = I

    D = {}
    D["xT0"] = dscr("s_xT0", [DM, SEQ])
    D["xT1"] = dscr("s_xT1", [DM, SEQ])
    D["xa"] = dscr("s_xa", [SEQ, DM], F32)
    D["xb"] = dscr("s_xb", [SEQ, DM], F32)
    D["memT"] = dscr("s_memT", [DM, 256])
    D["ab_w_in"] = dscr("s_w_ab_w_in", [DM, W_IN_COLS])
    D["cmp_peT"] = dscr("s_w_cmp_peT", [2, 64, 32])
    D["cmp_w1"] = dscr("s_w_cmp_w1", [2, 2048, 128])
    D["cmp_w2"] = dscr("s_w_cmp_w2", [2, 128, 64])
    D["pool_w"] = dscr("s_w_pool_w", [4, 128, 128])
    D["ab_w_out"] = dscr("s_w_ab_w_out", [DM, DM])
    D["sg_w_in"] = dscr("s_w_sg_w_in", [DM, 2 * DM])
    D["sg_w_out"] = dscr("s_w_sg_w_out", [DM, DM])
    D["mem_wq"] = [dscr(f"s_w_mem_wq{l}", [DM, DM]) for l in range(2)]
    D["mem_wkv"] = [dscr(f"s_w_mem_wkv{l}", [DM, 2 * DM]) for l in range(2)]
    D["mem_wo"] = [dscr(f"s_w_mem_wo{l}", [DM, DM]) for l in range(2)]
    D["ffn_up"] = [dscr(f"s_w_ffn_up{l}", [NJ, 128, 8, 256]) for l in range(2)]
    D["ffn_down"] = [dscr(f"s_w_ffn_down{l}", [DFF, DM]) for l in range(2)]
    D["fm"] = dscr("s_fm", [16, 64, SEQ])
    D["vt"] = dscr("s_vt", [NT, 128, 4 * 65])
    D["gt"] = dscr("s_gt", [NT, 128, 24], F32)
    D["kcT"] = dscr("s_kcT", [2, 64, 256])
    D["vc"] = dscr("s_vc", [2, 2, 128, 65])
    D["selT"] = dscr("s_selT", [2, 64, SEQ])
    D["mixT"] = dscr("s_mixT", [DM, SEQ])
    C.D = D

    C.banks = [S.ps(f"bank{i}", [128, 512], F32) for i in range(8)]
    C.bank_i = 0

    C.bank_cls = None
    C.bank_ci = {}

    def bank(cls=None):
        if cls is not None and C.bank_cls is not None:
            lst = C.bank_cls[cls]
            i = C.bank_ci.get(cls, 0)
            C.bank_ci[cls] = i + 1
            return C.banks[lst[i % len(lst)]]
        b = C.banks[C.bank_i % 8]
        C.bank_i += 1
        return b
    C.bank = bank

    ident_f = S.sb("ident_f", [128, 128], F32)
    C.ident_f = ident_f
    C.ident = S.sb("ident", [128, 128], BF16)
    S.memset(ident_f[:], 0.0, eng="gpsimd")
    S.generic("gpsimd", lambda e: e.affine_select(out=ident_f[:], in_=ident_f[:], pattern=[[-1, 128]],
                                                  compare_op=ALU.not_equal, fill=1.0, base=0, channel_multiplier=1),
              reads=[ident_f], writes=[ident_f])
    S.copy(C.ident[:], ident_f[:], eng="gpsimd")
    C.eps = S.sb("eps_t", [128, 1], F32)
    S.memset(C.eps[:], LN_EPS_C, eng="gpsimd")
    C.neghalf = S.sb("neghalf_t", [128, 1], F32)
    S.memset(C.neghalf[:], -0.5, eng="gpsimd")

    def castcopy(dst, src):
        dma(S, dst, src, eng="gpsimd", max_dma_last_dim=4096)
    castcopy_now = castcopy

    C.cast_pending = []
    C.drip_n = 1

    C.cast_tag = 0

    def castcopy_lazy(dst, src):
        C.cast_pending.append((dst, src, C.cast_tag))

    def drip(n=None, upto=None):
        for _ in range(C.drip_n if n is None else n):
            if C.cast_pending and (upto is None or C.cast_pending[0][2] <= upto):
                d_, s_, _t = C.cast_pending.pop(0)
                castcopy(d_, s_)
    C.drip = drip

    def drip_setup(steps):
        C.drip_n = max(1, -(-len(C.cast_pending) // max(1, int(steps * 0.6))))
    C.drip_setup = drip_setup

    def cast_for(ph, lazy=True):
        castcopy = castcopy_lazy if lazy else castcopy_now
        if ph == "A":
            for r in range(8):
                castcopy(D["ab_w_in"][r * 128:(r + 1) * 128, :], I["ab_w_in"][r * 128:(r + 1) * 128, :])
            castcopy(D["pool_w"], I["pool_w"])
            castcopy(D["cmp_peT"], I["cmp_peT"])
            for k in range(2):
                for r in range(4):
                    castcopy(D["cmp_w1"][k, r * 512:(r + 1) * 512, :], I["cmp_w1"][k, r * 512:(r + 1) * 512, :])
            castcopy(D["cmp_w2"], I["cmp_w2"])
            for r in range(4):
                castcopy(D["ab_w_out"][r * 256:(r + 1) * 256, :], I["ab_w_out"][r * 256:(r + 1) * 256, :])
        elif ph in ("M0", "M1"):
            l = int(ph[1])
            for r in range(4):
                sl = slice(r * 256, (r + 1) * 256)
                castcopy(D["mem_wq"][l][sl, :], I["mem_wq"][l, sl, :])
            for r in range(8):
                sl = slice(r * 128, (r + 1) * 128)
                castcopy(D["mem_wkv"][l][sl, :], I["mem_wkv"][l, sl, :])
            for r in range(4):
                sl = slice(r * 256, (r + 1) * 256)
                castcopy(D["mem_wo"][l][sl, :], I["mem_wo"][l, sl, :])
        elif ph in ("F0", "F1"):
            l = int(ph[1])
            for j in range(NJ):
                for h in range(2):
                    c0 = h * DFF + j * 128
                    castcopy(D["ffn_up"][l][j, :, :, h * 128:(h + 1) * 128],
                             I["ffn_w_up"][l, :, c0:c0 + 128].rearrange("(kc p) c -> p kc c", p=128))
            for r in range(NJ):
                sl = slice(r * 128, (r + 1) * 128)
                castcopy(D["ffn_down"][l][sl, :], I["ffn_w_down"][l, sl, :])
        elif ph == "G":
            for r in range(8):
                sl = slice(r * 128, (r + 1) * 128)
                castcopy(D["sg_w_in"][sl, :], I["sg_w_in"][sl, :])
            for r in range(4):
                sl = slice(r * 256, (r + 1) * 256)
                castcopy(D["sg_w_out"][sl, :], I["sg_w_out"][sl, :])

    for r in range(8):
        castcopy(D["xT0"][r * 128:(r + 1) * 128, :], I["xT"][r * 128:(r + 1) * 128, :])
    castcopy(D["memT"], I["memT"])
    S.barrier()
    cast_for(phases[0], lazy=False)
    for ph_ in phases[1:]:
        cast_for(ph_, lazy=(phases[0] == "A"))

    x_in, xT_in = I["x"], D["xT0"]
    xbufs = [D["xa"], D["xb"]]
    xTbufs = [D["xT1"], D["xT0"]]
    ln_idx = {"A": 0, "M0": 1, "F0": 2, "G": 3, "M1": 4, "F1": 5}
    for pi, ph in enumerate(phases):
        last = pi == len(phases) - 1
        x_out = out_d if last else xbufs[pi % 2]
        xT_out = None if last else xTbufs[pi % 2]
        li = ln_idx[ph]
        C.drip_setup(64)
        if ph == "A":
            phase_nsa(C, x_in, xT_in, x_out, xT_out, li)
        elif ph in ("M0", "M1"):
            phase_mem(C, int(ph[1]), x_in, xT_in, x_out, xT_out, li)
        elif ph in ("F0", "F1"):
            phase_ffn(C, int(ph[1]), x_in, xT_in, x_out, xT_out, li)
        elif ph == "G":
            phase_sgu(C, x_in, xT_in, x_out, xT_out, li)
        C.drip(10 ** 6)
        x_in, xT_in = x_out, xT_out
    S.emit()
    return S


def tail_alloc(C, li, g_eng="vector", d1=1, d2=1, d1b=1, norm_eng="scalar"):
    S = C.S
    T = Ctx()
    T.g_eng = g_eng
    T.gbc = S.sb("t_gbc", [128, DM], F32)
    T.bbc = S.sb("t_bbc", [128, DM], F32)
    dma(S, T.gbc[:], C.I["ln_g"][li:li + 1, :].partition_broadcast(128))
    dma(S, T.bbc[:], C.I["ln_b"][li:li + 1, :].partition_broadcast(128))
    T.NX = 3
    T.xt = [S.sb(f"t_xt{i}", [128, DM], F32) for i in range(T.NX)]
    T.z = [S.sb(f"t_z{i}", [128, DM], F32) for i in range(T.NX)]
    T.xb = [S.sb(f"t_xb{i}", [128, DM], BF16) for i in range(4)]
    T.st = [S.sb(f"t_st{i}", [128, 12], F32) for i in range(T.NX)]
    T.mv = [S.sb(f"t_mv{i}", [128, 4], F32) for i in range(T.NX)]
    T.xTblk = [S.sb(f"t_xTblk{i}", [128, 8, TB], BF16) for i in range(2)]
    T.pending = []
    T.d1, T.d2, T.d1b = d1, d2, d1b
    T.norm_eng = norm_eng
    return T


def tail_load_x(C, T, x_in, tile_i):
    dma(C.S, T.xt[tile_i % T.NX][:], x_in[tile_i * 128:(tile_i + 1) * 128, :])


def tail_tick(C, T):
    for it in T.pending:
        it[0] -= 1
    due = [it for it in T.pending if it[0] <= 0]
    T.pending = [it for it in T.pending if it[0] > 0]
    for it in due:
        it[1]()


def tail_flush(C, T, keep=0):
    while T.pending:
        T.pending.pop(0)[1]()


def tail_tile(C, T, ya, yb, tile_i, x_out, xT_out):
    S = C.S
    k = tile_i % T.NX
    xt, z, st, mv = T.xt[k], T.z[k], T.st[k], T.mv[k]
    xb = T.xb[tile_i % 4]
    S.stt(z[:, 0:512], xt[:, 0:512], ALPHA_C, ya[:], ALU.mult, ALU.add)
    S.stt(z[:, 512:1024], xt[:, 512:1024], ALPHA_C, yb[:], ALU.mult, ALU.add)
    S.generic("vector", lambda e: e.bn_stats(out=st[:, 0:6], in_=z[:, 0:512]), reads=[z], writes=[st])
    S.generic("vector", lambda e: e.bn_stats(out=st[:, 6:12], in_=z[:, 512:1024]), reads=[z], writes=[st])
    S.generic("vector", lambda e: e.bn_aggr(out=mv[:, 0:2], in_=st[:]), reads=[st], writes=[mv])
    S.ts(mv[:, 2:3], mv[:, 1:2], LN_EPS_C, None, op0=ALU.add)

    def part1():
        S.tt(mv[:, 2:3], mv[:, 2:3], C.neghalf[:], ALU.pow, eng="gpsimd")
        S.ts(mv[:, 3:4], mv[:, 0:1], mv[:, 2:3], -1.0, op0=ALU.mult, op1=ALU.mult)
        if T.d1b > 0:
            T.pending.append([T.d1b, part1b])
        else:
            part1b()

    def part1b():
        if T.norm_eng == "vector":
            S.ts(z[:], z[:], mv[:, 2:3], mv[:, 3:4], op0=ALU.mult, op1=ALU.add)
        else:
            S.act(z[:], z[:], AF.Identity, bias=mv[:, 3:4], scale=mv[:, 2:3])
        S.tt(z[:], z[:], T.gbc[:], ALU.mult, eng=T.g_eng)
        S.tt(z[:], z[:], T.bbc[:], ALU.add, eng="gpsimd")
        dma(S, x_out[tile_i * 128:(tile_i + 1) * 128, :], z[:])
        if xT_out is not None:
            T.pending.append([T.d2, part2])

    def part2():
        tb, s = tile_i // 4, tile_i % 4
        blk = T.xTblk[tb % 2]
        S.copy(xb[:], z[:], eng="scalar")
        pt = C.bank()
        ptb = pt[:].bitcast(BF16)
        for kc in range(8):
            S.tr(ptb[:, kc * 128:(kc + 1) * 128], xb[:, kc * 128:(kc + 1) * 128], C.ident[:])
        S.copy(blk[:, :, s * 128:(s + 1) * 128], ptb.rearrange("p (k t) -> p k t", k=8), eng="vector")
        if s == 3:
            dma(S, xT_out[:, tb * TB:(tb + 1) * TB].rearrange("(kc p) t -> p kc t", p=128), blk[:])
    T.pending.append([T.d1, part1])


def load_w(C, name, src, kc, n, eng="sync"):
    S = C.S
    t = S.sb(name, [128, kc, n], BF16)
    half = kc // 2 if kc >= 2 else kc
    dma(S, t[:, 0:half, :], src[0:half * 128, :].rearrange("(kc p) n -> p kc n", p=128), eng=eng)
    if half < kc:
        dma(S, t[:, half:kc, :], src[half * 128:kc * 128, :].rearrange("(kc p) n -> p kc n", p=128), eng=eng)
    return t


def load_xT_blk(C, dst, xT_in, tb):
    dma(C.S, dst[:], xT_in[:, tb * TB:(tb + 1) * TB].rearrange("(kc p) t -> p kc t", p=128))


def phase_mem(C, layer, x_in, xT_in, x_out, xT_out, li):
    S, D = C.S, C.D
    with S.scope():
        T = tail_alloc(C, li, g_eng="gpsimd")
        memT = load_w(C, "m_memT", D["memT"], 8, 256)
        wkv = load_w(C, "m_wkv", D["mem_wkv"][layer], 8, 2 * DM)
        wq = load_w(C, "m_wq", D["mem_wq"][layer], 8, DM, eng="scalar")
        wo = load_w(C, "m_wo", D["mem_wo"][layer], 8, DM, eng="scalar")
        KT = S.sb("m_KT", [128, 8, 256], BF16)
        V = S.sb("m_V", [128, 2, 4, 257], BF16)
        for mc in range(2):
            for h in range(4):
                S.memset(V[:, mc, h, 256:257], 1.0, eng="gpsimd")
        for hd in range(8):
            ps = C.bank()
            for kc in range(8):
                S.mm(ps[:, 0:256], wkv[:, kc, hd * 128:(hd + 1) * 128], memT[:, kc, :], start=kc == 0, stop=kc == 7)
            S.copy(KT[:, hd, :], ps[:, 0:256], eng="vector")
        for mc in range(2):
            for half in range(2):
                ps = C.bank()
                for kc in range(8):
                    S.mm(ps[:], memT[:, kc, mc * 128:(mc + 1) * 128],
                         wkv[:, kc, DM + half * 512:DM + (half + 1) * 512], start=kc == 0, stop=kc == 7)
                for hh in range(2):
                    S.copy(V[:, mc, 2 * half + hh, 0:256], ps[:, hh * 256:(hh + 1) * 256], eng="vector")
        xTb = [S.sb(f"m_xTb{i}", [128, 8, TB], BF16) for i in range(2)]
        qT = S.sb("m_qT", [128, 8, TB], BF16)
        eT = [S.sb(f"m_eT{i}", [128, TB], BF16) for i in range(4)]
        O = [S.sb(f"m_O{i}", [128, DM], BF16) for i in range(4)]
        OT = [S.sb(f"m_OT{i}", [128, 8, 128], BF16) for i in range(2)]
        rl = [S.sb(f"m_rl{i}", [128, 1], F32) for i in range(4)]
        qTs = [qT, S.sb("m_qT2", [128, 8, TB], BF16)]
        load_xT_blk(C, xTb[0], xT_in, 0)
        rli = [0]

        def stA(tb):
            if tb + 1 < NTB:
                load_xT_blk(C, xTb[(tb + 1) % 2], xT_in, tb + 1)
            xt = xTb[tb % 2]
            qq = qTs[tb % 2]
            for hd in range(8):
                ps = C.bank()
                for kc in range(8):
                    S.mm(ps[:], wq[:, kc, hd * 128:(hd + 1) * 128], xt[:, kc, :], start=kc == 0, stop=kc == 7)
                if hd % 2 == 0:
                    S.act(qq[:, hd, :], ps[:], AF.Copy, scale=1.0 / 16.0)
                else:
                    S.ts(qq[:, hd, :], ps[:], 1.0 / 16.0, None, op0=ALU.mult)

        def stB(tb):
            qq = qTs[tb % 2]

            def scores(h):
                es = []
                for mc in range(2):
                    ps = C.bank()
                    for dc in range(2):
                        S.mm(ps[:], KT[:, h * 2 + dc, mc * 128:(mc + 1) * 128], qq[:, h * 2 + dc, :],
                             start=dc == 0, stop=dc == 1)
                    e = eT[(h % 2) * 2 + mc]
                    S.act(e[:], ps[:], AF.Exp)
                    es.append(e)
                return es

            def pv(h, es):
                for s in range(4):
                    po = C.bank()
                    for mc in range(2):
                        S.mm(po[:, 0:257], es[mc][:, s * 128:(s + 1) * 128], V[:, mc, h, :], start=mc == 0, stop=mc == 1)
                    r = rl[rli[0] % 4]
                    rli[0] += 1
                    S.generic("vector", lambda e, r=r, po=po: e.reciprocal(r[:], po[:, 256:257]), reads=[po], writes=[r])
                    if s % 2 == 0:
                        S.act(O[s][:, h * 256:(h + 1) * 256], po[:, 0:256], AF.Copy, scale=r[:])
                    else:
                        S.ts(O[s][:, h * 256:(h + 1) * 256], po[:, 0:256], r[:], None, op0=ALU.mult)
            prev = None
            for h in range(4):
                es = scores(h)
                if prev is not None:
                    pv(*prev)
                prev = (h, es)
                if tb >= 1:
                    stC2(tb - 1, h)
            pv(*prev)

        OT4 = [S.sb(f"m_OT4_{i}", [128, 8, 128], BF16) for i in range(4)]

        def stC1(tb):
            for s in range(4):
                pt = C.bank()
                ptb = pt[:].bitcast(BF16)
                for kc in range(8):
                    S.tr(ptb[:, kc * 128:(kc + 1) * 128], O[s][:, kc * 128:(kc + 1) * 128], C.ident[:])
                S.copy(OT4[s][:], ptb.rearrange("p (k t) -> p k t", k=8), eng="vector" if s % 2 == 0 else "scalar")
            tail_load_x(C, T, x_in, tb * 4)
            tail_load_x(C, T, x_in, tb * 4 + 1)

        def stC2(tb, s):
            ti = tb * 4 + s
            ot = OT4[s]
            ya, yb = C.bank(), C.bank()
            for half, y in enumerate((ya, yb)):
                for kc in range(8):
                    S.mm(y[:], ot[:, kc, :], wo[:, kc, half * 512:(half + 1) * 512], start=kc == 0, stop=kc == 7)
            tail_tile(C, T, ya, yb, ti, x_out, xT_out)
            if s + 2 < 4:
                tail_load_x(C, T, x_in, tb * 4 + s + 2)
            tail_tick(C, T)

        stA(0)
        for tb in range(NTB):
            stB(tb)
            if tb + 1 < NTB:
                stA(tb + 1)
            stC1(tb)
        for s in range(4):
            stC2(NTB - 1, s)
        tail_flush(C, T)


def phase_ffn(C, layer, x_in, xT_in, x_out, xT_out, li):
    S, D = C.S, C.D
    with S.scope():
        T = tail_alloc(C, li, g_eng="gpsimd", d1=2, d2=4, d1b=0)
        wd = S.sb("f_wd", [128, NJ, DM], BF16)
        cwb = S.sb("f_cwb", [128, 2 * NJ, 4], F32)
        dma(S, cwb[:], C.I["ffn_cwb"][layer])
        cr_all = [S.sb(f"f_crall{i}", [128, 2 * NJ, 2], F32) for i in range(2)]
        bc = [S.sb(f"f_bc{i}", [128, 2 * NJ, 2], F32) for i in range(2)]
        btmp = S.sb("f_btmp", [128, 2 * NJ], F32)
        S.memset(cr_all[1][:], 0.0, eng="gpsimd")
        actT = [S.sb(f"f_actT{i}", [128, NJ, TB], BF16) for i in range(2)]
        NWU = 6
        wu = [S.sb(f"f_wu{i}", [128, 8, 256], BF16) for i in range(NWU)]
        NCV = 8
        cv = [S.sb(f"f_cv{i}", [128, TB], F32) for i in range(NCV)]
        sg = [S.sb(f"f_sg{i}", [128, TB], F32) for i in range(2)]
        xTb = [S.sb(f"f_xTb{i}", [128, 8, TB], BF16) for i in range(2)]
        load_xT_blk(C, xTb[0], xT_in, 0)
        n_w = NTB * NJ
        for i_ in range(4):
            dma(S, wu[i_][:], D["ffn_up"][layer][i_], eng="scalar")
        for q in range(2):
            dma(S, wd[:, q * 11:(q + 1) * 11, :],
                D["ffn_down"][layer][q * 11 * 128:(q + 1) * 11 * 128, :].rearrange("(j p) n -> p j n", p=128))

        def block_bias(tb):
            cr = cr_all[(tb + 1) % 2]
            o = bc[tb % 2]
            S.tt(o[:, :, 1], cr[:, :, 1], cwb[:, :, 0], ALU.mult)
            S.tt(btmp[:], cr[:, :, 1], cwb[:, :, 1], ALU.mult)
            S.tt(o[:, :, 0], cr[:, :, 0], cwb[:, :, 0], ALU.mult)
            S.tt(o[:, :, 0], o[:, :, 0], btmp[:], ALU.add)

        def stage1(tb, j):
            xt = xTb[tb % 2]
            wi = tb * NJ + j
            if wi + 4 < n_w:
                dma(S, wu[(wi + 4) % NWU][:], D["ffn_up"][layer][(wi + 4) % NJ], eng="scalar")
            w = wu[wi % NWU]
            for half in range(2):
                ps = C.bank()
                for kc in range(8):
                    S.mm(ps[:], w[:, kc, half * 128:(half + 1) * 128], xt[:, kc, :], start=kc == 0, stop=kc == 7)
                idx = half * NJ + j
                c = cv[(2 * j + half) % NCV]
                S.act(c[:], ps[:], AF.Identity, scale=cwb[:, idx, 2:3], bias=cwb[:, idx, 3:4])
                if tb + 1 < NTB:
                    S.copy(cr_all[tb % 2][:, idx, :], ps[:, TB - 2:TB], eng="scalar")
                S.stt(c[:, 1:TB], ps[:, 0:TB - 1], cwb[:, idx, 1:2], c[:, 1:TB], ALU.mult, ALU.add)
                S.stt(c[:, 2:TB], ps[:, 0:TB - 2], cwb[:, idx, 0:1], c[:, 2:TB], ALU.mult, ALU.add)
                if tb > 0:
                    S.tt(c[:, 0:2], c[:, 0:2], bc[tb % 2][:, idx, :], ALU.add, eng="gpsimd")

        def stage2(tb, j):
            sgt = sg[j % 2]
            S.act(sgt[:], cv[(2 * j + 1) % NCV][:], AF.Silu)
            S.tt(actT[tb % 2][:, j, :], cv[(2 * j) % NCV][:], sgt[:], ALU.mult, eng="gpsimd")

        def down(tb, s):
            at = actT[tb % 2]
            ya, yb = C.bank(), C.bank()
            for half, y in enumerate((ya, yb)):
                for j in range(NJ):
                    S.mm(y[:], at[:, j, s * 128:(s + 1) * 128], wd[:, j, half * 512:(half + 1) * 512],
                         start=j == 0, stop=j == NJ - 1)
            tail_tile(C, T, ya, yb, tb * 4 + s, x_out, xT_out)
            if tb * 4 + s + 2 < NT:
                tail_load_x(C, T, x_in, tb * 4 + s + 2)

        tail_load_x(C, T, x_in, 0)
        tail_load_x(C, T, x_in, 1)
        for tb in range(NTB + 1):
            if tb + 1 < NTB:
                load_xT_blk(C, xTb[(tb + 1) % 2], xT_in, tb + 1)
            if 0 < tb < NTB:
                block_bias(tb)
            for j in range(NJ):
                tail_tick(C, T)
                if tb < NTB:
                    stage1(tb, j)
                    if j >= 1:
                        stage2(tb, j - 1)
                if tb >= 1 and j % 5 == 2 and j // 5 < 4:
                    down(tb - 1, j // 5)
            if tb < NTB:
                stage2(tb, NJ - 1)
        tail_flush(C, T)


def phase_sgu(C, x_in, xT_in, x_out, xT_out, li):
    S, D, I = C.S, C.D, C.I
    with S.scope():
        T = tail_alloc(C, li, d1b=0, norm_eng="vector")
        win = load_w(C, "g_win", D["sg_w_in"], 8, 2 * DM)
        wout = load_w(C, "g_wout", D["sg_w_out"], 8, DM, eng="scalar")
        wsf = S.sb("g_wsf", [128, 8, 128], F32)
        dma(S, wsf[:], I["sg_w_sT"].rearrange("h s t -> s h t"))
        S.generic("gpsimd", lambda e: e.affine_select(out=wsf[:], in_=wsf[:], pattern=[[0, 8], [1, 128]],
                                                      compare_op=ALU.is_ge, fill=0.0, base=0, channel_multiplier=-1),
                  reads=[wsf], writes=[wsf])
        wsT = S.sb("g_wsT", [128, 8, 128], BF16)
        S.copy(wsT[:], wsf[:], eng="gpsimd")
        bsT = S.sb("g_bsT", [128, 8], F32)
        dma(S, bsT[:], I["sg_b_sT"])
        ng = S.sb("g_ng", [128, DM], F32)
        nb = S.sb("g_nb", [128, DM], F32)
        dma(S, ng[:], I["sg_norm_g"].partition_broadcast(128))
        dma(S, nb[:], I["sg_norm_b"].partition_broadcast(128))
        xTb = [S.sb(f"g_xTb{i}", [128, 8, TB], BF16) for i in range(2)]
        u = [S.sb(f"g_u{i}", [128, DM], F32) for i in range(3)]
        v = [S.sb(f"g_v{i}", [128, DM], F32) for i in range(2)]
        vb = [S.sb(f"g_vb{i}", [128, DM], BF16) for i in range(2)]
        sgo = [S.sb(f"g_sgo{i}", [128, DM], BF16) for i in range(2)]
        sgT = [S.sb(f"g_sgT{i}", [128, 8, 128], BF16) for i in range(2)]
        st = [S.sb(f"g_st{i}", [128, 12], F32) for i in range(2)]
        mv = [S.sb(f"g_mv{i}", [128, 4], F32) for i in range(2)]
        load_xT_blk(C, xTb[0], xT_in, 0)
        tail_load_x(C, T, x_in, 0)

        def g1(ti):
            tb, s = ti // 4, ti % 4
            k = ti % 2
            if s == 0 and tb + 1 < NTB:
                load_xT_blk(C, xTb[(tb + 1) % 2], xT_in, tb + 1)
            xt = xTb[tb % 2]
            pq = []
            for q in range(4):
                ps = C.bank()
                for kc in range(8):
                    S.mm(ps[:], xt[:, kc, s * 128:(s + 1) * 128], win[:, kc, q * 512:(q + 1) * 512],
                         start=kc == 0, stop=kc == 7)
                pq.append(ps)
            S.act(v[k][:, 0:512], pq[2][:], AF.Gelu_apprx_tanh)
            S.act(v[k][:, 512:1024], pq[3][:], AF.Gelu_apprx_tanh)
            S.act(u[ti % 3][:, 0:512], pq[0][:], AF.Gelu_apprx_tanh)
            S.act(u[ti % 3][:, 512:1024], pq[1][:], AF.Gelu_apprx_tanh)
            vv, stt_, mvv = v[k], st[k], mv[k]
            S.generic("vector", lambda e, vv=vv, stt_=stt_: e.bn_stats(out=stt_[:, 0:6], in_=vv[:, 0:512]), reads=[vv], writes=[stt_])
            S.generic("vector", lambda e, vv=vv, stt_=stt_: e.bn_stats(out=stt_[:, 6:12], in_=vv[:, 512:1024]), reads=[vv], writes=[stt_])
            S.generic("vector", lambda e, mvv=mvv, stt_=stt_: e.bn_aggr(out=mvv[:, 0:2], in_=stt_[:]), reads=[stt_], writes=[mvv])
            S.ts(mvv[:, 2:3], mvv[:, 1:2], LN_EPS_C, None, op0=ALU.add)
            S.tt(mvv[:, 2:3], mvv[:, 2:3], C.neghalf[:], ALU.pow, eng="gpsimd")
            S.ts(mvv[:, 3:4], mvv[:, 0:1], mvv[:, 2:3], -1.0, op0=ALU.mult, op1=ALU.mult)

        def g1b(ti):
            k = ti % 2
            vv, mvv = v[k], mv[k]
            S.act(vv[:], vv[:], AF.Identity, bias=mvv[:, 3:4], scale=mvv[:, 2:3])
            S.tt(vv[:], vv[:], ng[:], ALU.mult, eng="gpsimd")
            S.tt(vb[k][:], vv[:], nb[:], ALU.add, eng="gpsimd")

        def g2(ti):
            k = ti % 2
            p0, p1 = C.bank(), C.bank()
            for h in range(8):
                pp = (p0, p1)[h // 4]
                S.mm(pp[:, (h % 4) * 128:(h % 4 + 1) * 128], wsT[:, h, :], vb[k][:, h * 128:(h + 1) * 128],
                     start=True, stop=True)
            for h in range(8):
                pp = (p0, p1)[h // 4]
                S.stt(sgo[k][:, h * 128:(h + 1) * 128], pp[:, (h % 4) * 128:(h % 4 + 1) * 128], bsT[:, h:h + 1],
                      u[ti % 3][:, h * 128:(h + 1) * 128], ALU.add, ALU.mult)

        def g3(ti):
            k = ti % 2
            pt = C.bank()
            ptb = pt[:].bitcast(BF16)
            for kc in range(8):
                S.tr(ptb[:, kc * 128:(kc + 1) * 128], sgo[k][:, kc * 128:(kc + 1) * 128], C.ident[:])
            S.copy(sgT[k][:], ptb.rearrange("p (k t) -> p k t", k=8), eng="scalar")

        def g4(ti):
            k = ti % 2
            ya, yb = C.bank(), C.bank()
            for half, y in enumerate((ya, yb)):
                for kc in range(8):
                    S.mm(y[:], sgT[k][:, kc, :], wout[:, kc, half * 512:(half + 1) * 512], start=kc == 0, stop=kc == 7)
            tail_tile(C, T, ya, yb, ti, x_out, xT_out)
            if ti + 2 < NT:
                tail_load_x(C, T, x_in, ti + 2)
            tail_tick(C, T)

        tail_load_x(C, T, x_in, 1)
        for kk in range(NT + 4):
            if kk < NT:
                g1(kk)
            if 0 <= kk - 1 < NT:
                g1b(kk - 1)
            if 0 <= kk - 2 < NT:
                g2(kk - 2)
            if 0 <= kk - 3 < NT:
                g3(kk - 3)
            if 0 <= kk - 4 < NT:
                g4(kk - 4)
        tail_flush(C, T)


NSA_STOP = [0]


def phase_nsa(C, x_in, xT_in, x_out, xT_out, li):
    S, D, I = C.S, C.D, C.I
    FM = D["fm"]
    with S.scope():
        win = load_w(C, "a_win", D["ab_w_in"], 8, W_IN_COLS)
        pw = S.sb("a_pw", [128, 4, 128], BF16)
        dma(S, pw[:], D["pool_w"].rearrange("g c d -> c g d"))
        pscale = S.sb("a_pscale", [128, 4], F32)
        dma(S, pscale[:], I["pool_scaleT"])
        io_i = S.sb("a_ioi", [128, 16], I32)
        S.generic("gpsimd", lambda e: e.iota(io_i[:], pattern=[[1, 16]], base=1, channel_multiplier=0), writes=[io_i])
        io_f = S.sb("a_iof", [128, 16], F32)
        S.copy(io_f[:], io_i[:], eng="vector")
        rcnt = S.sb("a_rcnt", [128, 4, 16], F32)
        for gi in range(4):
            S.ts(rcnt[:, gi, :], io_f[:], float(2 ** (gi + 1)), None, op0=ALU.min)
            S.generic("vector", lambda e, gi=gi: e.reciprocal(rcnt[:, gi, :], rcnt[:, gi, :]), reads=[rcnt], writes=[rcnt])
        fm_cols = [(h * 128, 0.125) for h in range(4)]
        fm_cols += [(768, 1.0), (1024, 1.0), (512, 1.0), (640, 1.0)]
        xTb = [S.sb(f"a_xTb{i}", [128, 8, TB], BF16) for i in range(2)]
        fo = [S.sb(f"a_fo{i}", [128, 8, TB], BF16) for i in range(2)]
        ub = [[S.sb(f"a_ub{gi}_{i}", [128, 16 + TB], F32) for i in range(2)] for gi in range(4)]
        wk = [S.sb(f"a_wk{i}", [128, 16 + TB], F32) for i in range(2)]
        pl = [S.sb(f"a_pl{i}", [128, TB], BF16) for i in range(4)]
        ptmp = S.sb("a_ptmp", [128, 16], F32)
        mo = [S.sb(f"a_mo{i}", [128, 4, TB], BF16) for i in range(2)]
        vt = [S.sb(f"a_vt{i}", [128, 4, 65], BF16) for i in range(2)]
        gtt = [S.sb(f"a_gt{i}", [128, 24], F32) for i in range(2)]
        for i in range(2):
            S.memset(vt[i][:, :, 64:65], 1.0, eng="gpsimd")
        for gi in range(4):
            S.memset(ub[gi][1][:, TB:TB + 16], 0.0, eng="gpsimd")
        for i in range(2):
            S.memset(wk[i][:], 0.0, eng="gpsimd")
        load_xT_blk(C, xTb[0], xT_in, 0)
        for tb in range(NTB):
            if tb + 1 < NTB:
                load_xT_blk(C, xTb[(tb + 1) % 2], xT_in, tb + 1)
            xt = xTb[tb % 2]
            f = fo[tb % 2]
            for gi in range(4):
                ps = C.bank()
                c0 = 1304 + gi * 128
                for kc in range(8):
                    S.mm(ps[:], win[:, kc, c0:c0 + 128], xt[:, kc, :], start=kc == 0, stop=kc == 7)
                u = ub[gi][tb % 2]
                up = ub[gi][(tb + 1) % 2]
                S.copy(u[:, 0:16], up[:, TB:TB + 16], eng="gpsimd")
                S.act(u[:, 16:16 + TB], ps[:], AF.Copy)
                cur = u
                W = 16 + TB
                for k in range(gi + 1):
                    sh = 2 ** k
                    nxt = wk[k % 2]
                    S.tt(nxt[:, sh:W], cur[:, sh:W], cur[:, 0:W - sh], ALU.add)
                    cur = nxt
                p = pl[gi]
                S.stt(p[:], cur[:, 16:W], 1.0 / (2 ** (gi + 1)), u[:, 16:W], ALU.mult, ALU.subtract)
                if tb == 0:
                    S.tt(ptmp[:], cur[:, 16:32], rcnt[:, gi, :], ALU.mult)
                    S.tt(p[:, 0:16], ptmp[:], u[:, 16:32], ALU.subtract)
            for gi_, (c0, sc) in enumerate(fm_cols):
                ps = C.bank()
                for kc in range(8):
                    S.mm(ps[:], win[:, kc, c0:c0 + 128], xt[:, kc, :], start=kc == 0, stop=kc == 7)
                if True:
                    S.act(f[:, gi_, :], ps[:], AF.Copy, scale=sc)
                else:
                    S.ts(f[:, gi_, :], ps[:], sc, None, op0=ALU.mult)
            for two in range(2):
                dma(S, FM[:, :, tb * TB:(tb + 1) * TB].rearrange("(pr two) p t -> two p pr t", two=2)[two],
                    f[two * 64:(two + 1) * 64, :, :])
            m = mo[tb % 2]
            for gi in range(4):
                py = C.bank()
                S.mm(py[:], pw[:, gi, :], pl[gi][:], start=True, stop=True)
                S.ts(m[:, gi, :], py[:], pscale[:, gi:gi + 1], None, op0=ALU.mult)
            dma(S, D["mixT"][512:1024, tb * TB:(tb + 1) * TB].rearrange("(g p) t -> p g t", p=128), m[:])
            for s in range(4):
                ti = tb * 4 + s
                ps = C.bank()
                for kc in range(8):
                    S.mm(ps[:, 0:128], xt[:, kc, s * 128:(s + 1) * 128], win[:, kc, 896:1024], start=kc == 0, stop=kc == 7)
                for kc in range(8):
                    S.mm(ps[:, 128:280], xt[:, kc, s * 128:(s + 1) * 128], win[:, kc, 1152:1304], start=kc == 0, stop=kc == 7)
                v = vt[ti % 2]
                gt = gtt[ti % 2]
                S.copy(v[:, :, 0:64], ps[:, 0:256].rearrange("p (a d) -> p a d", a=4), eng="vector")
                S.copy(gt[:], ps[:, 256:280], eng="vector")
                dma(S, D["vt"][ti], v[:].rearrange("p a d -> p (a d)"))
                dma(S, D["gt"][ti], gt[:])

    if NSA_STOP[0] == 1:
        return
    with S.scope():
        pe1 = S.sb("b_pe1", [64, 2, 32], BF16)
        dma(S, pe1[:], D["cmp_peT"].rearrange("k d l -> d k l"))
        w1 = S.sb("b_w1", [64, 2, 32, 128], BF16)
        for kvi in range(2):
            dma(S, w1[:, kvi, :, :], D["cmp_w1"][kvi].rearrange("(l d) h -> d l h", d=64), eng="scalar")
        c0 = S.sb("b_c0", [64, 4, SEQ], BF16)
        for q in range(4):
            dma(S, c0[:, q, :], FM[12 + q])
        pe2 = S.sb("b_pe2", [64, 2, 32, 2], BF16)
        for kvi in range(2):
            for dd in range(2):
                S.copy(pe2[:, kvi, :, dd], pe1[:, kvi, :], eng="vector")
        w2 = S.sb("b_w2", [128, 2, 64], BF16)
        dma(S, w2[:], D["cmp_w2"].rearrange("k h d -> h k d"))
        cb = S.sb("b_cb", [128, 2], F32)
        for kvi in range(2):
            pb = C.bank()
            for l in range(32):
                S.mm(pb[:, 0:2], w1[:, kvi, l, :], pe2[:, kvi, l, :], start=l == 0, stop=l == 31)
            S.copy(cb[:, kvi:kvi + 1], pb[:, 0:1], eng="vector")
        hid = [S.sb(f"b_hid{i}", [128, 256], BF16) for i in range(4)]
        kcs = S.sb("b_kcs", [64, 2, 256], BF16)
        vcs = S.sb("b_vcs", [128, 4, 65], BF16)
        S.memset(vcs[:, :, 64:65], 1.0, eng="gpsimd")
        for kvi in range(2):
            for g in range(2):
                ph = C.bank()
                for l in range(32):
                    S.mm(ph[:, 0:255], w1[:, kvi, l, :], c0[:, kvi * 2 + g, l:l + 4065:16], start=l == 0, stop=l == 31)
                hd = hid[kvi * 2 + g]
                S.memset(hd[:, 255:256], 0.0, eng="gpsimd")
                S.act(hd[:, 0:255], ph[:, 0:255], AF.Gelu_apprx_tanh, bias=cb[:, kvi:kvi + 1])
                if kvi == 0:
                    pk = C.bank()
                    S.mm(pk[0:64, 0:256], w2[:, 0, :], hd[:], start=True, stop=True)
                    S.copy(kcs[:, g, :], pk[0:64, 0:256], eng="vector")
                else:
                    for n_ in range(2):
                        pv = C.bank()
                        S.mm(pv[:, 0:64], hd[:, n_ * 128:(n_ + 1) * 128], w2[:, 1, :], start=True, stop=True)
                        S.copy(vcs[:, g * 2 + n_, 0:64], pv[:, 0:64], eng="vector")
        dma(S, D["kcT"].rearrange("g p n -> p g n"), kcs[:])
        dma(S, D["vc"].rearrange("g n p c -> p (g n) c"), vcs[:])

    if NSA_STOP[0] == 2:
        return
    with S.scope():
        kTs = S.sb("d_kTs", [128, 4, SEQ], BF16)
        S.memset(kTs[64:128, :, :], 0.0, eng="vector")
        kc_ = S.sb("c_kc", [128, 2, 256], BF16)
        S.memset(kc_[64:128, :, :], 0.0, eng="gpsimd")
        dma(S, kc_[0:64, :, :], D["kcT"].rearrange("g p n -> p g n"))
        q8 = [S.sb(f"d_q8{i}", [128, 8, TB], BF16) for i in range(2)]
        for i in range(2):
            S.memset(q8[i][64:128, :, :], 0.0, eng="gpsimd")

        def load_q8(dst, tb):
            dma(S, dst[0:64, :, :], FM[0:8, :, tb * TB:(tb + 1) * TB].rearrange("h p t -> p h t"))
        load_q8(q8[0], 0)
        load_q8(q8[1], 1)
        zf = S.sb("c_zf", [128, 504], F32)
        S.memset(zf[:], 0.0, eng="gpsimd")
        S.generic("gpsimd", lambda e: e.affine_select(out=zf[:], in_=zf[:], pattern=[[-16, 504]], compare_op=ALU.is_ge,
                                                      fill=NEGB, base=16 * 248 - 31, channel_multiplier=1),
                  reads=[zf], writes=[zf])
        Tcb = S.sb("c_Tcb", [128, 504], BF16)
        S.copy(Tcb[:], zf[:], eng="gpsimd")
        Tf = S.sb("c_Tf", [128, 128], F32)
        S.memset(Tf[:], 0.0, eng="gpsimd")
        S.memset(Tf[0:64, 63:65], 100.0, eng="gpsimd")
        S.memset(Tf[0:64, 65:128], -1e30, eng="gpsimd")
        S.memset(Tf[64:128, 64:66], 100.0, eng="gpsimd")
        S.memset(Tf[64:128, 66:128], -1e30, eng="gpsimd")
        P = [S.sb(f"c_P{i}", [128, 4, 256], F32) for i in range(2)]
        nm = [S.sb(f"c_nm{i}", [128, 4], F32) for i in range(2)]
        sm = [S.sb(f"c_sm{i}", [128, 4], F32) for i in range(2)]
        imp = [S.sb(f"c_imp{i}", [128, 64], F32) for i in range(2)]
        imp2 = [S.sb(f"c_imp2{i}", [128, 64], F32) for i in range(2)]
        sc_ = [S.sb(f"c_sc{i}", [128, 64], F32) for i in range(2)]
        wk_ = [S.sb(f"c_wk{i}", [128, 64], F32) for i in range(2)]
        m8 = [S.sb(f"c_m8{i}", [128, 16], F32) for i in range(2)]
        selm = [S.sb(f"c_selm{i}", [128, 64], BF16) for i in range(2)]
        selQ = [S.sb(f"d_selQ{i}", [128, 2, TB], BF16) for i in range(NTB)]
        for i in range(NTB):
            S.memset(selQ[i][64:128, :, :], 0.0, eng="gpsimd")
        it = [0]
        a3_pend = []
        C.bank_cls = {"s": [0, 1, 2], "po": [4], "pq": [3, 5], "a3": [6, 7]}

        def a3_flush(keep=0):
            while len(a3_pend) > keep:
                a3_pend.pop(0)()

        def a3_unit(ti, g, q):
            tb, s = ti // 4, ti % 4
            k = it[0] % 2
            it[0] += 1
            pS = [C.bank("a3"), C.bank("a3")]
            for r in range(4):
                o = pS[r // 2][:, (r % 2) * 256:(r % 2 + 1) * 256]
                S.mm(o, q[:, 4 * g + r, s * 128:(s + 1) * 128], kc_[:, g, :], start=True, stop=False)
                S.mm(o, C.ident[:], Tcb[:, 248 - 8 * ti:504 - 8 * ti], start=False, stop=True)
            Pk, nmk, smk = P[k], nm[k], sm[k]
            for r in range(4):
                o = pS[r // 2][:, (r % 2) * 256:(r % 2 + 1) * 256]
                S.act(Pk[:, r, :], o, AF.Exp, accum_out=smk[:, r:r + 1])
            S.ts(smk[:], smk[:], 1e-30, None, op0=ALU.add)
            S.generic("vector", lambda e, smk=smk: e.reciprocal(smk[:], smk[:]), reads=[smk], writes=[smk])
            for r in range(4):
                S.ts(Pk[:, r, :], Pk[:, r, :], smk[:, r:r + 1], None, op0=ALU.mult)
            ik, i2k, sk, wkk, mk, slk = imp[k], imp2[k], sc_[k], wk_[k], m8[k], selm[k]
            S.generic("vector", lambda e, ik=ik, Pk=Pk: e.tensor_reduce(
                out=ik[:], in_=Pk[:].rearrange("p r (j k) -> p j r k", k=4), axis=AX.XY, op=ALU.add),
                reads=[Pk], writes=[ik])
            S.generic("vector", lambda e, i2k=i2k, Pk=Pk: e.tensor_reduce(
                out=i2k[:], in_=Pk[:, :, 3:256:4].rearrange("p r j -> p j r"), axis=AX.X, op=ALU.add),
                reads=[Pk], writes=[i2k])
            S.tt(ik[:, 1:64], ik[:, 1:64], i2k[:, 0:63], ALU.add)
            S.tt(sk[:], ik[:], Tf[:, 64 - 2 * ti:128 - 2 * ti], ALU.add)
            S.ts(sk[:, 0:1], sk[:, 0:1], 100.0, None, op0=ALU.add)
            S.generic("vector", lambda e, mk=mk, sk=sk: e.max(out=mk[:, 0:8], in_=sk[:]), reads=[sk], writes=[mk])
            S.generic("vector", lambda e, mk=mk, sk=sk, wkk=wkk: e.match_replace(
                out=wkk[:], in_to_replace=mk[:, 0:8], in_values=sk[:], imm_value=-1e30),
                reads=[sk, mk], writes=[wkk])
            S.generic("vector", lambda e, mk=mk, wkk=wkk: e.max(out=mk[:, 8:16], in_=wkk[:]), reads=[wkk], writes=[mk])
            S.ts(slk[:], sk[:], mk[:, 15:16], -1.0, op0=ALU.is_ge, op1=ALU.add)

            def fin(slk=slk, tb=tb, g=g, s=s):
                pt = C.bank("pq")
                ptb = pt[:].bitcast(BF16)
                S.tr(ptb[0:64, 0:128], slk[:], C.ident[:])
                S.copy(selQ[tb][0:64, g, s * 128:(s + 1) * 128], ptb[0:64, 0:128], eng="vector")
            a3_pend.append(fin)

        vcs = S.sb("d_vcs", [128, 4, 65], BF16)
        dma(S, vcs[:], D["vc"].rearrange("g n p c -> p (g n) c"))
        vts = S.sb("d_vts", [128, NT, 260], BF16)
        dma(S, vts[:], D["vt"].rearrange("i p c -> p i c"))
        gts = S.sb("d_gts", [128, NT, 24], F32)
        dma(S, gts[:], D["gt"].rearrange("i p c -> p i c"))
        for q_ in range(4):
            dma(S, kTs[0:64, q_, :], FM[8 + q_])
        zb = S.sb("d_zb", [128, TB], BF16)
        S.memset(zb[:], 0.0, eng="gpsimd")
        caus = S.sb("d_caus", [128, 4, TB], BF16)
        band = S.sb("d_band", [128, 4, TB], BF16)
        for d_ in range(4):
            S.generic("gpsimd", lambda e, d_=d_: e.affine_select(out=caus[:, d_, :], in_=zb[:], pattern=[[1, TB]],
                                                                 compare_op=ALU.is_ge, fill=NEGB, base=-128 * d_,
                                                                 channel_multiplier=-1), reads=[zb], writes=[caus])
            S.generic("gpsimd", lambda e, d_=d_: e.affine_select(out=band[:, d_, :], in_=zb[:], pattern=[[-1, TB]],
                                                                 compare_op=ALU.is_ge, fill=NEGB, base=128 * d_ - 1,
                                                                 channel_multiplier=1), reads=[zb], writes=[band])
        cmpb = {}
        for (qb, c) in [(q_, 0) for q_ in range(5)] + [(q_, 1) for q_ in range(4, 8)]:
            t_ = S.sb(f"d_cmpb{qb}_{c}", [128, TB], BF16)
            S.generic("gpsimd", lambda e, t_=t_, qb=qb, c=c: e.affine_select(
                out=t_[:], in_=zb[:], pattern=[[1, TB]], compare_op=ALU.is_ge, fill=NEGB,
                base=512 * qb - 2048 * c - 31, channel_multiplier=-16), reads=[zb], writes=[t_])
            cmpb[(qb, c)] = t_
        Wide = S.sb("d_Wide", [128, SEQ], BF16)
        S.memset(Wide[64:128, :], 0.0, eng="gpsimd")
        S.memset(Wide[0:64, :], -NEGB, eng="gpsimd")
        S.generic("gpsimd", lambda e: e.affine_select(out=Wide[0:64, :], in_=Wide[0:64, :], pattern=[[1, SEQ]], compare_op=ALU.is_ge,
                                                      fill=0.0, base=0, channel_multiplier=-64), reads=[Wide], writes=[Wide])
        S.generic("gpsimd", lambda e: e.affine_select(out=Wide[0:64, :], in_=Wide[0:64, :], pattern=[[-1, SEQ]], compare_op=ALU.is_ge,
                                                      fill=0.0, base=63, channel_multiplier=64), reads=[Wide], writes=[Wide])
        NE = 42
        epool = [S.sb(f"d_e{i}", [128, TB], BF16) for i in range(NE)]
        ei = 0
        mx = [[S.sb(f"d_mx{i}_{s}", [128, 512], BF16) for s in range(4)] for i in range(2)]
        acc = [S.sb(f"d_acc{i}", [128, 64], F32) for i in range(8)]
        rl4 = [S.sb(f"d_rl{i}", [128, 4], F32) for i in range(4)]
        mT = [S.sb(f"d_mT{i}", [128, 4, TB], BF16) for i in range(2)]

        S.act(gts[:], gts[:], AF.Sigmoid)
        ri = [0]

        oT_sb = [S.sb(f"d_oT{i}", [65, TB], F32) for i in range(4)]
        oi = [0]

        def pv_unit(qb, h, br, es):
            po = C.bank("po")
            n = len(es)
            for ui, (c, e, v, lo_, hi_) in enumerate(es):
                S.mm(po[0:65, lo_:hi_], v, e[:, lo_:hi_], start=ui == 0, stop=ui == n - 1, skip_group_check=True)
            osb = oT_sb[oi[0] % 4]
            oi[0] += 1
            S.copy(osb[:], po[0:65, :], eng="scalar" if qb < 4 else "vector")
            return (qb, h, br, osb)

        def fin_unit(qb, h, br, osb):
            po = C.bank("pq")
            for s in range(4):
                S.tr(po[:, s * 65:(s + 1) * 65], osb[:, s * 128:(s + 1) * 128], C.ident_f[0:65, 0:65])
            r4 = rl4[ri[0] % 4]
            ri[0] += 1
            S.ts(r4[:], po[:, 64:260:65], 1e-30, None, op0=ALU.add)
            S.generic("vector", lambda e, r4=r4: e.reciprocal(r4[:], r4[:]), reads=[r4], writes=[r4])
            S.tt(r4[:], r4[:], gts[:, 4 * qb:4 * qb + 4, h * 3 + br], ALU.mult)
            for s in range(4):
                a = acc[(h % 2) * 4 + s]
                dst = mx[qb % 2][s][:, h * 64:(h + 1) * 64] if br == 2 else a[:]
                if br == 0:
                    S.ts(dst, po[:, s * 65:s * 65 + 64], r4[:, s:s + 1], None, op0=ALU.mult)
                else:
                    S.stt(dst, po[:, s * 65:s * 65 + 64], r4[:, s:s + 1], a[:], ALU.mult, ALU.add)

        for qb in range(NTB):
            if 1 <= qb and qb + 1 < NTB:
                load_q8(q8[(qb + 1) % 2], qb + 1)
            q = q8[qb % 2]
            pend = None
            fin_q = []
            for h in range(8):
                g = h // 4
                for br in range(3):
                    chunks = []
                    if br == 0:
                        for c in range(2):
                            if c == 1 and qb < 4:
                                continue
                            bl = []
                            if (qb, c) in cmpb:
                                bl.append((C.ident[:], cmpb[(qb, c)][:]))
                            chunks.append((c, kc_[:, g, c * 128:(c + 1) * 128], vcs[:, g * 2 + c, :], bl, 0, TB))
                    elif br == 1:
                        for c in range(4 * qb + 4):
                            bl = []
                            if qb >= 2:
                                bl.append((Wide[:, c * 128:(c + 1) * 128], selQ[qb][:, g, :]))
                            lo_ = 0
                            if c >= 4 * qb:
                                bl.append((C.ident[:], caus[:, c - 4 * qb, :]))
                                lo_ = (c - 4 * qb) * 128
                            chunks.append((c, kTs[:, g, c * 128:(c + 1) * 128], vts[:, c, g * 65:(g + 1) * 65], bl, lo_, TB))
                    else:
                        for c in range(max(0, 4 * qb - 4), 4 * qb + 4):
                            lo_, hi_ = 0, TB
                            if c >= 4 * qb:
                                bl = [(C.ident[:], caus[:, c - 4 * qb, :])]
                                lo_ = (c - 4 * qb) * 128
                            else:
                                bl = [(C.ident[:], band[:, c - (4 * qb - 4), :])]
                                hi_ = (c - (4 * qb - 4) + 1) * 128
                            chunks.append((c, kTs[:, 2 + g, c * 128:(c + 1) * 128], vts[:, c, (2 + g) * 65:(3 + g) * 65], bl, lo_, hi_))
                    es = []
                    for (c, kT, v, bl, lo_, hi_) in chunks:
                        ps = C.bank("s")
                        S.mm(ps[:, lo_:hi_], kT, q[:, h, lo_:hi_], start=True, stop=len(bl) == 0)
                        for bi, (l_, r_) in enumerate(bl):
                            S.mm(ps[:, lo_:hi_], l_, r_[:, lo_:hi_], start=False, stop=bi == len(bl) - 1)
                        e = epool[ei % NE]
                        ei += 1
                        S.act(e[:, lo_:hi_], ps[:, lo_:hi_], AF.Exp)
                        es.append((c, e, v, lo_, hi_))
                    if len(fin_q) >= 2:
                        fin_unit(*fin_q.pop(0))
                    if pend is not None:
                        fin_q.append(pv_unit(*pend))
                    pend = (qb, h, br, es)
                C.drip()
                a3_flush()
                if 2 <= qb + 1 < NTB:
                    a3_unit((qb + 1) * 4 + h // 2, h % 2, q8[(qb + 1) % 2])
            fin_q.append(pv_unit(*pend))
            while fin_q:
                fin_unit(*fin_q.pop(0))
            a3_flush()
            m_ = mT[qb % 2]
            for s in range(4):
                pt = C.bank("pq")
                ptb = pt[:].bitcast(BF16)
                for kc in range(4):
                    S.tr(ptb[:, kc * 128:(kc + 1) * 128], mx[qb % 2][s][:, kc * 128:(kc + 1) * 128], C.ident[:])
                S.copy(m_[:, :, s * 128:(s + 1) * 128], ptb[:, 0:512].rearrange("p (k t) -> p k t", k=4), eng="vector")
            dma(S, D["mixT"][0:512, qb * TB:(qb + 1) * TB].rearrange("(kc p) t -> p kc t", p=128), m_[:])
        C.bank_cls = None

    if NSA_STOP[0] == 4:
        return
    with S.scope():
        T = tail_alloc(C, li, d1b=0)
        wout = load_w(C, "e_wout", D["ab_w_out"], 8, DM)
        mb = [S.sb(f"e_mb{i}", [128, 8, TB], BF16) for i in range(2)]
        load_xT_blk(C, mb[0], D["mixT"], 0)
        tail_load_x(C, T, x_in, 0)
        tail_load_x(C, T, x_in, 1)
        for tb in range(NTB):
            if tb + 1 < NTB:
                load_xT_blk(C, mb[(tb + 1) % 2], D["mixT"], tb + 1)
            for s in range(4):
                ti = tb * 4 + s
                ya, yb = C.bank(), C.bank()
                for half, y in enumerate((ya, yb)):
                    for kc in range(8):
                        S.mm(y[:], mb[tb % 2][:, kc, s * 128:(s + 1) * 128], wout[:, kc, half * 512:(half + 1) * 512],
                             start=kc == 0, stop=kc == 7)
                tail_tile(C, T, ya, yb, ti, x_out, xT_out)
                if ti + 2 < NT:
                    tail_load_x(C, T, x_in, ti + 2)
                tail_tick(C, T)
        tail_flush(C, T)


PHASES_ALL = ["A", "M0", "F0", "G", "M1", "F1"]
_CACHE = {}


def host_inputs(inp, b):
    f = np.float32
    c = np.ascontiguousarray
    m = {}
    m["x"] = c(inp["x"][b], dtype=f)
    m["xT"] = c(inp["x"][b].T, dtype=f)
    m["memT"] = c(inp["mem"][b].T, dtype=f)
    m["ab_w_in"] = c(inp["ab_w_in"][0], dtype=f)
    m["cmp_peT"] = c(np.transpose(inp["nsa_cmp_pe"][0], (0, 2, 1)), dtype=f)
    m["cmp_w1"] = c(inp["nsa_cmp_w1"][0], dtype=f)
    m["cmp_w2"] = c(inp["nsa_cmp_w2"][0], dtype=f)
    m["pool_w"] = c(inp["pool_w"][0], dtype=f)
    m["pool_scaleT"] = c(inp["pool_scale"][0].reshape(4, 128).T, dtype=f)
    m["ab_w_out"] = c(inp["ab_w_out"][0], dtype=f)
    m["sg_w_in"] = c(inp["sg_w_in"][0], dtype=f)
    m["sg_norm_g"] = c(inp["sg_norm_g"][0:1], dtype=f)
    m["sg_norm_b"] = c(inp["sg_norm_b"][0:1], dtype=f)
    m["sg_w_sT"] = c(np.transpose(inp["sg_w_s"][0], (0, 2, 1)), dtype=f)
    m["sg_b_sT"] = c(inp["sg_b_s"][0].T, dtype=f)
    m["sg_w_out"] = c(inp["sg_w_out"][0], dtype=f)
    m["ln_g"] = c(inp["ln_g"].reshape(6, DM), dtype=f)
    m["ln_b"] = c(inp["ln_b"].reshape(6, DM), dtype=f)
    m["mem_wq"] = c(inp["mem_wq"], dtype=f)
    m["mem_wkv"] = c(inp["mem_wkv"], dtype=f)
    m["mem_wo"] = c(inp["mem_wo"], dtype=f)
    m["ffn_w_up"] = c(inp["ffn_w_up"], dtype=f)
    cw = np.concatenate([np.transpose(inp["ffn_conv_w"], (0, 2, 1)), inp["ffn_conv_b"][:, :, None]], axis=2)
    m["ffn_cwb"] = c(np.transpose(cw.reshape(2, 2 * NJ, 128, 4), (0, 2, 1, 3)), dtype=f)
    m["ffn_w_down"] = c(inp["ffn_w_down"], dtype=f)
    return m


def kernel(**inputs):
    inp = {k: np.asarray(v) for k, v in inputs.items()}
    if "nc" not in _CACHE:
        nc = bass.Bass("TRN2", target_bir_lowering=False)
        build_program(nc, PHASES_ALL)
        _CACHE["nc"] = nc
    nc = _CACHE["nc"]
    shared = host_inputs(inp, 0)
    in_maps = []
    for b in range(8):
        m = dict(shared)
        m["x"] = np.ascontiguousarray(inp["x"][b], dtype=np.float32)
        m["xT"] = np.ascontiguousarray(inp["x"][b].T, dtype=np.float32)
        m["memT"] = np.ascontiguousarray(inp["mem"][b].T, dtype=np.float32)
        in_maps.append(m)
    res = run_bass_kernel_spmd(nc, in_maps, core_ids=list(range(8)))
    return np.stack([np.asarray(r["out"], dtype=np.float32) for r in res.results], axis=0)
```
